# Optimizing a Trainium2 kernel written in Bass

```python
import jax, jax.numpy as jnp
from jax import lax
import numpy as np

D_MODEL = 2048
BATCH = 2
SEQ = 8192
DEPTH = 1

N_ATTN_HEADS = 8
HEAD_DIM = 128
ATTN_WIDTH = N_ATTN_HEADS * HEAD_DIM
CONV_GROUPS = 8
CONV_WIDTH = CONV_GROUPS * 128
CONV_K = 3
MOBA_BLOCK = 256
MOBA_TOPK = 3
Q_CHUNK = 32
ROPE_THETA = 500000.0
ROPE_DIM = HEAD_DIM // 4
N_MEM = 256
N_XATTN_HEADS = 4
XATTN_WIDTH = N_XATTN_HEADS * HEAD_DIM
D_FF = 4 * D_MODEL
NORM_EPS = 1e-6
IN_PROJ_WIDTH = 3 * ATTN_WIDTH + 3 * CONV_WIDTH + 2 * D_MODEL

kernel_name = "hybrid_moba_shortconv_gated_block"


def rms_norm(x, g):
    xf = x.astype(jnp.float32)
    y = xf * lax.rsqrt(jnp.mean(xf * xf, axis=-1, keepdims=True) + NORM_EPS)
    return (y * g.astype(jnp.float32)).astype(x.dtype)


def partial_rope(t, pos):
    half = ROPE_DIM // 2
    inv_freq = ROPE_THETA ** (-jnp.arange(half, dtype=jnp.float32) / half)
    ang = pos.astype(jnp.float32)[:, None] * inv_freq[None, :]
    cos, sin = jnp.cos(ang), jnp.sin(ang)
    tr = t[..., :ROPE_DIM].astype(jnp.float32)
    t1, t2 = tr[..., :half], tr[..., half:]
    rot = jnp.concatenate([t1 * cos - t2 * sin, t2 * cos + t1 * sin], axis=-1).astype(t.dtype)
    return jnp.concatenate([rot, t[..., ROPE_DIM:]], axis=-1)


def moba_attention(q, k, v):
    B, H, S, Dh = q.shape
    nb = -(-S // MOBA_BLOCK)
    s_pad = nb * MOBA_BLOCK
    n_ch = s_pad // Q_CHUNK
    padw = ((0, 0), (0, 0), (0, s_pad - S), (0, 0))
    q, k, v = (jnp.pad(t, padw) for t in (q, k, v))
    kb = k.reshape(B, H, nb, MOBA_BLOCK, Dh)
    vb = v.reshape(B, H, nb, MOBA_BLOCK, Dh)
    k_mean = jnp.mean(kb.astype(jnp.float32), axis=3)
    gate = jnp.einsum('bhsd,bhnd->bhsn', q.astype(jnp.float32), k_mean)
    q_blk = jnp.arange(s_pad) // MOBA_BLOCK
    fully_past = jnp.arange(nb)[None, :] < q_blk[:, None]
    gate = jnp.where(fully_past, gate, -jnp.inf)
    top_val, top_idx = lax.top_k(gate, min(MOBA_TOPK, nb))
    own = jnp.broadcast_to(q_blk[:, None], (B, H, s_pad, 1)).astype(top_idx.dtype)
    blk_idx = jnp.concatenate([top_idx, own], axis=-1)
    blk_ok = jnp.concatenate([jnp.isfinite(top_val), jnp.ones((B, H, s_pad, 1), dtype=bool)], axis=-1)

    def to_chunks(t):
        return jnp.moveaxis(t.reshape((B, H, n_ch, Q_CHUNK) + t.shape[3:]), 2, 0)

    qpos = jnp.arange(s_pad).reshape(n_ch, Q_CHUNK)
    offs = jnp.arange(MOBA_BLOCK)
    scale = HEAD_DIM ** -0.5
    gather = jax.vmap(jax.vmap(lambda tb, ib: tb[ib]))

    def attend_chunk(args):
        qc, idx, ok, qp = args
        kg = gather(kb, idx)
        vg = gather(vb, idx)
        s = jnp.einsum('bhqd,bhqnkd->bhqnk', qc, kg).astype(jnp.float32) * scale
        kpos = idx[..., None] * MOBA_BLOCK + offs
        mask = ok[..., None] & (kpos <= qp[:, None, None])
        s = jnp.where(mask, s, -jnp.inf)
        p = jax.nn.softmax(s.reshape(B, H, Q_CHUNK, -1), axis=-1).reshape(s.shape)
        return jnp.einsum('bhqnk,bhqnkd->bhqd', p.astype(vg.dtype), vg)

    out = lax.map(attend_chunk, (to_chunks(q), to_chunks(blk_idx), to_chunks(blk_ok), qpos))
    return jnp.moveaxis(out, 0, 2).reshape(B, H, s_pad, Dh)[:, :, :S]


def short_conv(u, w):
    S = u.shape[1]
    up = jnp.pad(u, ((0, 0), (CONV_K - 1, 0), (0, 0)))
    y = up[:, 0:S, :] * w[0]
    for j in range(1, CONV_K):
        y = y + up[:, j:j + S, :] * w[j]
    return y


def split_in_proj(proj):
    widths = [ATTN_WIDTH] * 3 + [CONV_WIDTH] * 3 + [D_MODEL, D_MODEL]
    cuts, acc = [], 0
    for w in widths[:-1]:
        acc += w
        cuts.append(acc)
    return jnp.split(proj, cuts, axis=-1)


def heads(t, n_heads):
    B, S, _ = t.shape
    return t.reshape(B, S, n_heads, HEAD_DIM).transpose(0, 2, 1, 3)


def merge_heads(t):
    B, H, S, Dh = t.shape
    return t.transpose(0, 2, 1, 3).reshape(B, S, H * Dh)


def hybrid_layer(x, mem, norm_mix, w_in, conv_w, w_attn_out, w_conv_out, w_mix_out,
                 norm_xattn, norm_mem, wq_x, wkv_x, wo_x, norm_mlp, w_up, w_down):
    B, S, _ = x.shape
    pos = jnp.arange(S)
    h = rms_norm(x, norm_mix)
    q, k, v, cx, cb, cc, ga, gc = split_in_proj(h @ w_in)
    q = partial_rope(heads(q, N_ATTN_HEADS), pos)
    k = partial_rope(heads(k, N_ATTN_HEADS), pos)
    y_attn = merge_heads(moba_attention(q, k, heads(v, N_ATTN_HEADS))) @ w_attn_out
    y_conv = (cb * short_conv(cc * cx, conv_w)) @ w_conv_out
    merged = jax.nn.sigmoid(ga) * y_attn + jax.nn.sigmoid(gc) * y_conv
    x = x + merged @ w_mix_out
    hq = heads(rms_norm(x, norm_xattn) @ wq_x, N_XATTN_HEADS)
    mk, mv = jnp.split(rms_norm(mem, norm_mem) @ wkv_x, 2, axis=-1)
    mk, mv = heads(mk, N_XATTN_HEADS), heads(mv, N_XATTN_HEADS)
    s = jnp.einsum('bhsd,bhmd->bhsm', hq, mk).astype(jnp.float32) * (HEAD_DIM ** -0.5)
    p = jax.nn.softmax(s, axis=-1).astype(mv.dtype)
    x = x + merge_heads(jnp.einsum('bhsm,bhmd->bhsd', p, mv)) @ wo_x
    u = jax.nn.relu(rms_norm(x, norm_mlp) @ w_up)
    return x + (u * u) @ w_down


def setup_inputs(seed: int = 0) -> dict:
    key = jax.random.key(seed)
    ks = jax.random.split(key, 20)

    def dense(k, shape, fan_in):
        return jax.random.normal(k, shape, jnp.float32) * (fan_in ** -0.5)

    def gain(k, shape):
        return 1.0 + 0.01 * jax.random.normal(k, shape, jnp.float32)

    L, D = DEPTH, D_MODEL
    return {
        "x": jax.random.normal(ks[0], (BATCH, SEQ, D), jnp.float32),
        "mem": jax.random.normal(ks[1], (BATCH, N_MEM, D), jnp.float32),
        "norm_mix": gain(ks[2], (L, D)),
        "w_in": dense(ks[3], (L, D, IN_PROJ_WIDTH), D),
        "conv_w": dense(ks[4], (L, CONV_K, CONV_WIDTH), CONV_K),
        "w_attn_out": dense(ks[5], (L, ATTN_WIDTH, D), ATTN_WIDTH),
        "w_conv_out": dense(ks[6], (L, CONV_WIDTH, D), CONV_WIDTH),
        "w_mix_out": dense(ks[7], (L, D, D), D),
        "norm_xattn": gain(ks[8], (L, D)),
        "norm_mem": gain(ks[9], (L, D)),
        "wq_x": dense(ks[10], (L, D, XATTN_WIDTH), D),
        "wkv_x": dense(ks[11], (L, D, 2 * XATTN_WIDTH), D),
        "wo_x": dense(ks[12], (L, XATTN_WIDTH, D), XATTN_WIDTH),
        "norm_mlp": gain(ks[13], (L, D)),
        "w_up": dense(ks[14], (L, D, D_FF), D),
        "w_down": dense(ks[15], (L, D_FF, D), D_FF),
        "norm_final": gain(ks[16], (D,)),
    }


def reference(x, mem, norm_mix, w_in, conv_w, w_attn_out, w_conv_out, w_mix_out,
              norm_xattn, norm_mem, wq_x, wkv_x, wo_x, norm_mlp, w_up, w_down, norm_final):
    for l in range(DEPTH):
        x = hybrid_layer(x, mem, norm_mix[l], w_in[l], conv_w[l], w_attn_out[l], w_conv_out[l],
                         w_mix_out[l], norm_xattn[l], norm_mem[l], wq_x[l], wkv_x[l], wo_x[l],
                         norm_mlp[l], w_up[l], w_down[l])
    return rms_norm(x, norm_final)
```

```python
import contextlib

import numpy as np

import concourse.bass as bass
import concourse.mybir as mybir
from concourse.bass_utils import run_bass_kernel_spmd

F32 = mybir.dt.float32
BF16 = mybir.dt.bfloat16
ALU = mybir.AluOpType
AF = mybir.ActivationFunctionType
AX = mybir.AxisListType

D = 2048
KC = 16
NCTX = 8192
NOWN = 2048
H = 8
DH = 128
NBLK = 32
NPAST = 24
XH = 4
DFF = 8192
EPS = 1e-6
NEG = -1.0e30
SCALE = DH ** -0.5


class Ins:
    __slots__ = ("eng", "emit", "deps", "mark", "cnt", "is_dma", "dsem", "dval", "gid")

    def __init__(self, eng, emit, is_dma=False):
        self.eng = eng
        self.emit = emit
        self.deps = []
        self.mark = False
        self.cnt = 0
        self.is_dma = is_dma
        self.dsem = None
        self.dval = 0
        self.gid = 0


class Planner:
    ENGS = ("pe", "act", "dve", "pool", "sp")

    def __init__(self, n_dma_sems=12):
        self.streams = {e: [] for e in self.ENGS}
        self.reg = {}
        self.n_dma_sems = n_dma_sems
        self.dma_rr = {e: 0 for e in self.ENGS}
        self.dma_last = {}
        self.dma_uses = {}
        self.all = []
        self.open_dmas = []

    def _add(self, ins, reads, writes):
        deps = set()
        for k in reads:
            st = self.reg.get(k)
            if st is not None and st[0] is not None:
                deps.add(st[0])
            if st is not None and isinstance(k, str) and k.startswith("p"):
                deps.update(v for en, v in st[1].items() if en != ins.eng)
        for k in writes:
            st = self.reg.get(k)
            if st is not None:
                if st[0] is not None:
                    deps.add(st[0])
                deps.update(st[1].values())
                deps.update(st[2])
        if ins.eng == "pe" and not ins.is_dma:
            deps = {d for d in deps if not (d.eng == "pe" and not d.is_dma)}
        deps.discard(ins)
        ins.deps = list(deps)
        for k in reads:
            st = self.reg.setdefault(k, [None, {}, []])
            if ins.is_dma:
                st[2].append(ins)
            else:
                st[1][ins.eng] = ins
        for k in writes:
            self.reg[k] = [ins, {}, []]
        ins.gid = len(self.all)
        self.all.append(ins)
        self.streams[ins.eng].append(ins)
        return ins

    def op(self, eng, emit, reads=(), writes=()):
        return self._add(Ins(eng, emit), reads, writes)

    def dma(self, eng, emit, reads=(), writes=(), after=()):
        ins = Ins(eng, emit, is_dma=True)
        slot = self.dma_rr[eng] % self.n_dma_sems
        self.dma_rr[eng] += 1
        key = (eng, slot)
        n = self.dma_uses.get(key, 0) + 1
        self.dma_uses[key] = n
        ins.dsem = key
        ins.dval = 16 * n
        self._add(ins, reads, writes)
        ins.deps.extend(after)
        prev = self.dma_last.get(key)
        if prev is not None:
            ins.deps.append(prev)
        self.dma_last[key] = ins
        self.open_dmas.append(ins)
        return ins

    def barrier(self):
        deps = []
        for st in self.streams.values():
            for ins in reversed(st):
                if not ins.is_dma and ins.emit is not None:
                    deps.append(ins)
                    break
        deps = deps + list(self.open_dmas)
        self.open_dmas = []
        for e in self.ENGS:
            ins = Ins(e, None)
            ins.deps = [d for d in deps]
            ins.gid = len(self.all)
            self.all.append(ins)
            self.streams[e].append(ins)
        self.reg = {}

    def finalize(self):
        for ins in self.all:
            for d in ins.deps:
                d.mark = True
        for e in self.ENGS:
            c = 0
            for ins in self.streams[e]:
                if ins.mark and not ins.is_dma:
                    c += 1
                    ins.cnt = c

    def emit_engine(self, eng, e, esems, dsems):
        known = {}
        stream = self.streams[eng]
        look = 40 if eng == "pe" else 0

        def kv(d):
            if d.is_dma:
                return ("d",) + d.dsem, d.dval
            return ("e", d.eng), d.cnt

        for pos, ins in enumerate(stream):
            need = {}
            for d in ins.deps:
                key, val = kv(d)
                if need.get(key, 0) < val:
                    need[key] = val
            for key, val in need.items():
                if known.get(key, 0) >= val:
                    continue
                for nxt in stream[pos + 1:pos + 1 + look]:
                    for d in nxt.deps:
                        if d.gid < ins.gid and not d.is_dma:
                            k2, v2 = kv(d)
                            if k2 == key and v2 > val:
                                val = v2
                known[key] = val
                sem = dsems[key[1:]] if key[0] == "d" else esems[key[1]]
                e.wait_ge(sem, val)
            if ins.emit is None:
                continue
            bi = ins.emit(e)
            if ins.is_dma:
                bi.then_inc(dsems[ins.dsem], 16)
            elif ins.mark:
                bi.then_inc(esems[eng], 1)
        for (de, slot), n in self.dma_uses.items():
            if de == eng:
                key = ("d", de, slot)
                if known.get(key, 0) < 16 * n:
                    e.wait_ge(dsems[(de, slot)], 16 * n)


class Rot:
    def __init__(self, items):
        self.items = items
        self.i = 0

    def next(self):
        it = self.items[self.i % len(self.items)]
        self.i += 1
        return it


def build(debug=False, phases="MATB"):
    nc = bass.Bass("TRN2", target_bir_lowering=False)

    def din(name, shape, dt=F32):
        return nc.dram_tensor(name, list(shape), dt, kind="ExternalInput").ap()

    xc = din("xc", [NCTX, D])
    mem = din("mem", [256, D])
    w_in = din("w_in", [D, 10240])
    w_ao = din("w_attn_out", [1024, D])
    w_co = din("w_conv_out", [1024, D])
    w_mix = din("w_mix_out", [D, D])
    wq_x = din("wq_x", [D, 512])
    wkv_x = din("wkv_x", [D, 1024])
    wo_x = din("wo_x", [512, D])
    w_up = din("w_up", [D, DFF])
    w_down = din("w_down", [DFF, D])
    gbc_d = din("gbc", [2, 128, D])
    gT_d = din("gT", [128, 4, KC])
    cwT_d = din("cwT", [128, 8, 3])
    rope_d = din("rope", [NCTX, 128])
    gbias_d = din("gbias", [128, 16, NBLK])
    ident_d = din("ident", [128, 128])
    tri_d = din("tri", [128, 128])

    skind = "ExternalOutput" if debug else "Internal"
    kT_scr = nc.dram_tensor("kT_scr", [H, 32, 128, 256], BF16, kind=skind).ap()
    v_scr = nc.dram_tensor("v_scr", [NCTX, 1024], BF16, kind=skind).ap()
    qT_scr = nc.dram_tensor("qT_scr", [H, 8, 128, 256], BF16, kind=skind).ap()
    at_scr = nc.dram_tensor("at_scr", [1024, NOWN], BF16, kind=skind).ap()
    out_d = nc.dram_tensor("out", [NOWN, D], F32, kind="ExternalOutput").ap()
    wscr = nc.dram_tensor("wscr", [59, 128, 8192], BF16, kind="Internal").ap()

    def panel_list():
        L = []
        for c0 in (3072, 5120, 4096):
            for pg in range(2):
                L.append([(w_in, 0, KC, c0 + pg * 512, 512, 0)])
        for cgp in range(4):
            L.append([(w_in, 0, KC, 6144 + cgp * 512, 512, 0)])
            L.append([(w_in, 0, KC, 8192 + cgp * 512, 512, 0)])
            L.append([(w_ao, 0, 8, cgp * 512, 512, 0), (w_co, 0, 8, cgp * 512, 512, 4096)])
        for cgp in range(4):
            L.append([(w_mix, 0, KC, cgp * 512, 512, 0)])
        L.append([(wq_x, 0, KC, 0, 512, 0)])
        for cgp in range(4):
            L.append([(wo_x, 0, XH, cgp * 512, 512, 0)])
        for half in range(2):
            for pg in range(8):
                L.append([(w_up, 0, KC, half * 4096 + pg * 512, 512, 0)])
            for cgp in range(4):
                for qd in range(2):
                    L.append([(w_down, half * 4096 + qd * 2048, KC, cgp * 512, 512, 0)])
        return L

    PANELS = panel_list()
    CASTS = [(slot,) + part for slot, parts in enumerate(PANELS) for part in parts]
    cast_i = [0]

    P = Planner()
    es = contextlib.ExitStack()
    with es:
        def sb(name, shape, dt, stack=es):
            return stack.enter_context(nc.sbuf_tensor("s_" + name, list(shape), dt))

        idf = sb("idf", [128, 128], F32)
        idb = sb("idb", [128, 128], BF16)
        trib = sb("trib", [128, 128], BF16)
        trif = sb("trif", [128, 128], F32)
        onesb = sb("onesb", [128, 128], BF16)
        gT = sb("gT", [128, 4, KC], F32)
        cwT = sb("cwT", [128, 8, 3], F32)
        mkT = sb("mkT", [128, XH, 256], BF16)
        mv = sb("mv", [128, 2, 512], BF16)
        hTh = sb("hTh", [128, KC, 2], BF16)
        stat = sb("stat", [128, 256], F32)
        junk = sb("junk", [128, D], BF16)

        P.dma("sp", lambda e: e.dma_start(out=idf[:], in_=ident_d), writes=["idf"])
        P.dma("sp", lambda e: e.dma_start(out=trif[:], in_=tri_d), writes=["trif"])
        P.dma("sp", lambda e: e.dma_start(out=gT[:], in_=gT_d), writes=["gT"])
        P.dma("sp", lambda e: e.dma_start(out=cwT[:], in_=cwT_d), writes=["cwT"])
        P.op("dve", lambda e: e.tensor_copy(out=idb[:], in_=idf[:]), reads=["idf"], writes=["idb"])
        P.op("dve", lambda e: e.tensor_copy(out=trib[:], in_=trif[:]), reads=["trif"], writes=["trib"])
        P.op("dve", lambda e: e.memset(onesb[:], 1.0), writes=["onesb"])
        P.op("dve", lambda e: e.memset(stat[:], 0.0), writes=["statinit"])

        stat_i = [0]
        cp_i = [0]

        def emit_cast(after):
            if cast_i[0] >= len(CASTS):
                return
            slot, wap, r0, kcn, c0, ncol, off = CASTS[cast_i[0]]
            cast_i[0] += 1
            src = wap[r0:r0 + kcn * 128, c0:c0 + ncol].rearrange("(kc p) n -> p kc n", p=128)
            dst = wscr[slot, :, off:off + kcn * ncol].rearrange("p (k n) -> p k n", k=kcn)
            P.dma("pool", lambda e: e.dma_start(out=dst, in_=src), writes=[("wscr", slot, off)], after=after)

        def copy_op(out, in_, reads, writes):
            cp_i[0] += 1
            if cp_i[0] % 2:
                P.op("act", lambda e: e.activation(out=out, in_=in_, func=AF.Copy), reads=reads, writes=writes)
            else:
                P.op("dve", lambda e: e.tensor_copy(out=out, in_=in_), reads=reads, writes=writes)

        def norm_tm(xs, kx, gbc, kg, hb, khb):
            i = stat_i[0]
            stat_i[0] += 1
            ks = ("stat", i)
            c0 = 3 * i
            P.op("act", lambda e: e.activation(out=junk[:], in_=xs, func=AF.Square,
                                               accum_out=stat[:, c0:c0 + 1]),
                 reads=[kx, "statinit"], writes=[ks])
            P.op("dve", lambda e: e.tensor_scalar(out=stat[:, c0 + 1:c0 + 2], in0=stat[:, c0:c0 + 1],
                                                  scalar1=1.0 / D, scalar2=EPS, op0=ALU.mult, op1=ALU.add),
                 reads=[ks], writes=[ks])
            P.op("act", lambda e: e.activation(out=stat[:, c0 + 2:c0 + 3], in_=stat[:, c0 + 1:c0 + 2],
                                               func=AF.Sqrt), reads=[ks], writes=[ks])
            P.op("dve", lambda e: e.reciprocal(out=stat[:, c0 + 2:c0 + 3], in_=stat[:, c0 + 2:c0 + 3]),
                 reads=[ks], writes=[ks])
            P.op("dve", lambda e: e.scalar_tensor_tensor(out=hb, in0=xs, scalar=stat[:, c0 + 2:c0 + 3],
                                                         in1=gbc, op0=ALU.mult, op1=ALU.mult),
                 reads=[kx, ks, kg], writes=[khb])

        def transpose_bf(src, ksrc, n, dst3, kdst, pbrot):
            for b0 in range(0, n, 8):
                nb = min(8, n - b0)
                pb, kpb = pbrot.next()
                for k in range(nb):
                    P.op("pe", lambda e, k=k, pb=pb, b0=b0: e.transpose(out=pb[:, k * 128:(k + 1) * 128],
                                                                  in_=src[:, (b0 + k) * 128:(b0 + k + 1) * 128],
                                                                  identity=idb[:]),
                         reads=[ksrc, "idb"], writes=[kpb])
                copy_op(dst3[:, b0:b0 + nb, :], pb[:, 0:nb * 128].rearrange("p (k q) -> p k q", k=nb),
                        [kpb], [kdst])

        def load_w(dst, wap, r0, kcn, c0, ncol, key):
            src = wap[r0:r0 + kcn * 128, c0:c0 + ncol].rearrange("(kc p) n -> p kc n", p=128)
            P.dma("pool", lambda e: e.dma_start(out=dst, in_=src), writes=[key])

        if "M" in phases or "A" in phases:
            with contextlib.ExitStack() as sa:
                wq = sb("wq", [128, KC, 1024], BF16, sa)
                wk = sb("wk", [128, KC, 1024], BF16, sa)
                wv = sb("wv", [128, KC, 1024], BF16, sa)
                gbc = sb("gbc", [128, 2, D], F32, sa)
                xs = [sb("xs%d" % i, [128, D], F32, sa) for i in range(2)]
                hb = [sb("hb%d" % i, [128, D], BF16, sa) for i in range(2)]
                hT = [sb("hT%d" % i, [128, KC, 128], BF16, sa) for i in range(2)]
                vb = [sb("vb%d" % i, [128, 1024], BF16, sa) for i in range(2)]
                kb = [sb("kb%d" % i, [128, 1024], BF16, sa) for i in range(2)]
                qb = [sb("qb%d" % i, [128, 1024], BF16, sa) for i in range(2)]
                kTs = [sb("kTs%d" % i, [128, H, 256], BF16, sa) for i in range(2)]
                qTs = [sb("qTs%d" % i, [128, H, 256], BF16, sa) for i in range(2)]
                ropet = [sb("ropet%d" % i, [128, 128], F32, sa) for i in range(2)]
                rtmp = [sb("rtmp%d" % i, [128, 256], F32, sa) for i in range(4)]
                psf = [sa.enter_context(nc.psum_tensor("psfA%d" % i, [128, 512], F32)) for i in range(6)]
                psb = [sa.enter_context(nc.psum_tensor("psbA%d" % i, [128, 1024], BF16)) for i in range(2)]
                pfrot = Rot([(p, "pfA%d" % i) for i, p in enumerate(psf)])
                pbrot = Rot([(p, "pbA%d" % i) for i, p in enumerate(psb)])
                rtrot = Rot([(p, "rtmp%d" % i) for i, p in enumerate(rtmp)])

                P.dma("sp", lambda e: e.dma_start(out=gbc[:], in_=gbc_d.rearrange("g p d -> p g d")),
                      writes=["gbc"])
                for cg in range(2):
                    load_w(wk[:, :, cg * 512:(cg + 1) * 512], wkv_x, 0, KC, cg * 512, 512, ("wk", cg))

                hmT = sb("hmT", [128, KC, 256], BF16, sa)
                import os
                KSTOP = int(os.environ.get("KSTOP", "99"))
                for mt in range(2):
                    s = mt % 2
                    P.dma("sp", lambda e, s=s, mt=mt: e.dma_start(out=xs[s][:], in_=mem[mt * 128:(mt + 1) * 128, :]),
                          writes=[("xs", s)])
                    if KSTOP >= 2:
                        norm_tm(xs[s][:], ("xs", s), gbc[:, 1, :], "gbc", hb[s][:], ("hb", s))
                    if KSTOP >= 3:
                        transpose_bf(hb[s], ("hb", s), KC, hmT[:, :, mt * 128:(mt + 1) * 128], ("hmT", mt), pbrot)
                for hd in range(XH if KSTOP >= 4 else 0):
                    ps, kps = pfrot.next()
                    for kc in range(KC):
                        P.op("pe", lambda e, ps=ps, kc=kc, hd=hd: e.matmul(
                            ps[:, 0:256], wk[:, kc, hd * 128:(hd + 1) * 128], hmT[:, kc, :],
                            start=(kc == 0), stop=(kc == KC - 1)),
                            reads=[("wk", 0), ("hmT", 0), ("hmT", 1)], writes=[kps])
                    copy_op(mkT[:, hd, :], ps[:, 0:256], [kps], ["mkT"])
                for mt in range(2 if KSTOP >= 5 else 0):
                    ps, kps = pfrot.next()
                    for kc in range(KC):
                        P.op("pe", lambda e, ps=ps, kc=kc, mt=mt: e.matmul(
                            ps[:, :], hmT[:, kc, mt * 128:(mt + 1) * 128], wk[:, kc, 512:1024],
                            start=(kc == 0), stop=(kc == KC - 1)),
                            reads=[("wk", 1), ("hmT", mt)], writes=[kps])
                    copy_op(mv[:, mt, :], ps[:, :], [kps], ["mv"])

                if "A" in phases:
                    for cg in range(2):
                        load_w(wk[:, :, cg * 512:(cg + 1) * 512], w_in, 0, KC, 1024 + cg * 512, 512, ("wk", cg))
                        load_w(wv[:, :, cg * 512:(cg + 1) * 512], w_in, 0, KC, 2048 + cg * 512, 512, ("wv", cg))
                    for cg in range(2):
                        load_w(wq[:, :, cg * 512:(cg + 1) * 512], w_in, 0, KC, cg * 512, 512, ("wq", cg))
                    def load_x(t):
                        s = t % 2
                        P.dma("sp", lambda e: e.dma_start(out=xs[s][:], in_=xc[t * 128:(t + 1) * 128, :]),
                              writes=[("xs", s)])

                    def load_rope(t):
                        s = t % 2
                        P.dma("sp", lambda e: e.dma_start(out=ropet[s][:], in_=rope_d[t * 128:(t + 1) * 128, :]),
                              writes=[("ropet", s)])

                    def rope_evac(ps, kps, dstb, kdst, cg, s):
                        psv = ps[:, :].rearrange("p (h d) -> p h d", h=4)
                        dv = dstb[:, cg * 512:(cg + 1) * 512].rearrange("p (h d) -> p h d", h=4)
                        cos4 = ropet[s][:, 0:64].rearrange("p (h d) -> p h d", h=4)
                        sin4 = ropet[s][:, 64:128].rearrange("p (h d) -> p h d", h=4)
                        rt_, krt = rtrot.next()
                        rt = [rt_[:, 64 * i:64 * (i + 1)].rearrange("p (h d) -> p h d", h=4) for i in range(4)]
                        P.op("dve", lambda e: e.tensor_copy(out=dv[:, :, 32:128], in_=psv[:, :, 32:128]),
                             reads=[kps], writes=[kdst])
                        rk = ("ropet", s)
                        P.op("dve", lambda e: e.tensor_tensor(out=rt[0], in0=psv[:, :, 0:16], in1=cos4, op=ALU.mult),
                             reads=[kps, rk], writes=[krt])
                        P.op("dve", lambda e: e.tensor_tensor(out=rt[1], in0=psv[:, :, 16:32], in1=sin4, op=ALU.mult),
                             reads=[kps, rk], writes=[krt])
                        P.op("dve", lambda e: e.tensor_tensor(out=rt[2], in0=psv[:, :, 16:32], in1=cos4, op=ALU.mult),
                             reads=[kps, rk], writes=[krt])
                        P.op("dve", lambda e: e.tensor_tensor(out=rt[3], in0=psv[:, :, 0:16], in1=sin4, op=ALU.mult),
                             reads=[kps, rk], writes=[krt])
                        P.op("dve", lambda e: e.tensor_tensor(out=dv[:, :, 0:16], in0=rt[0], in1=rt[1], op=ALU.subtract),
                             reads=[krt], writes=[kdst])
                        P.op("dve", lambda e: e.tensor_tensor(out=dv[:, :, 16:32], in0=rt[2], in1=rt[3], op=ALU.add),
                             reads=[krt], writes=[kdst])

                    def proj(t, wt, kw, evac):
                        s = t % 2
                        for cg in range(2):
                            ps, kps = pfrot.next()
                            for kc in range(KC):
                                P.op("pe", lambda e, ps=ps, kc=kc, cg=cg: e.matmul(
                                    ps[:, :], hT[s][:, kc, :], wt[:, kc, cg * 512:(cg + 1) * 512],
                                    start=(kc == 0), stop=(kc == KC - 1)),
                                    reads=[(kw, cg), ("hT", s)], writes=[kps])
                            evac(ps, kps, cg)

                    def do_tile(t):
                        s = t % 2
                        own = t >= 48
                        transpose_bf(hb[s], ("hb", s), KC, hT[s][:, :, :], ("hT", s), pbrot)
                        if t % 2 == 1:
                            emit_cast([P.all[-1]])
                        if t + 1 < 64:
                            s1 = (t + 1) % 2
                            norm_tm(xs[s1][:], ("xs", s1), gbc[:, 0, :], "gbc", hb[s1][:], ("hb", s1))
                        if t + 2 < 64:
                            load_x(t + 2)
                        if t == 47:
                            P.op("dve", lambda e, s=s: e.tensor_copy(out=hTh[:, :, :], in_=hT[s][:, :, 126:128]),
                                 reads=[("hT", s)], writes=["hTh"])
                        g2 = (t // 2) % 2
                        proj(t, wk, "wk", lambda ps, kps, cg, s=s: rope_evac(ps, kps, kb[s], ("kb", s), cg, s))
                        if own:
                            proj(t, wq, "wq", lambda ps, kps, cg, s=s: rope_evac(ps, kps, qb[s], ("qb", s), cg, s))
                        proj(t, wv, "wv", lambda ps, kps, cg, s=s: copy_op(
                            vb[s][:, cg * 512:(cg + 1) * 512], ps[:, :], [kps], [("vb", s)]))
                        P.dma("sp", lambda e, s=s, t=t: e.dma_start(out=v_scr[t * 128:(t + 1) * 128, :], in_=vb[s][:]),
                              reads=[("vb", s)])
                        transpose_bf(kb[s], ("kb", s), H, kTs[g2][:, :, (t % 2) * 128:(t % 2 + 1) * 128],
                                     ("kTs", g2), pbrot)
                        if t % 2 == 1:
                            grp = t // 2
                            P.dma("sp", lambda e, g2=g2, grp=grp: e.dma_start(
                                out=kT_scr[:, grp:grp + 1, :, :].rearrange("h g d t -> d (h g) t"), in_=kTs[g2][:]),
                                reads=[("kTs", g2)])
                        if own:
                            transpose_bf(qb[s], ("qb", s), H, qTs[g2][:, :, (t % 2) * 128:(t % 2 + 1) * 128],
                                         ("qTs", g2), pbrot)
                            if t % 2 == 1:
                                grp = (t - 48) // 2
                                P.dma("sp", lambda e, g2=g2, grp=grp: e.dma_start(
                                    out=qT_scr[:, grp:grp + 1, :, :].rearrange("h g d t -> d (h g) t"), in_=qTs[g2][:]),
                                    reads=[("qTs", g2)])

                    load_x(0)
                    load_rope(0)
                    load_x(1)
                    load_rope(1)
                    norm_tm(xs[0][:], ("xs", 0), gbc[:, 0, :], "gbc", hb[0][:], ("hb", 0))
                    for t in range(64):
                        do_tile(t)
                        if t + 2 < 64:
                            load_rope(t + 2)
                P.barrier()

        if "T" in phases:
            with contextlib.ExitStack() as sa:
                kT = [sb("kT%d" % i, [128, NBLK, 256], BF16, sa) for i in range(2)]
                Vt = [sb("Vt%d" % i, [128, 64, 130], BF16, sa) for i in range(2)]
                qT = [sb("qT%d" % i, [128, NOWN], BF16, sa) for i in range(2)]
                gbias = sb("gbias", [128, 16, NBLK], F32, sa)
                kms = sb("kms", [128, NBLK], F32, sa)
                kmb = sb("kmb", [128, NBLK], BF16, sa)
                gm = sb("gm", [128, 16, NBLK], F32, sa)
                m8 = sb("m8", [128, 16, 8], F32, sa)
                thr = sb("thr", [128, 16], F32, sa)
                sel = [sb("sel%d" % i, [128, 16, NBLK], F32, sa) for i in range(2)]
                pt = [sb("pt%d" % i, [128, 512], BF16, sa) for i in range(6)]
                pd = [sb("pd%d" % i, [128, 128], BF16, sa) for i in range(4)]
                oacc = [sb("oacc%d" % i, [128, 4, 129], F32, sa) for i in range(2)]
                rec = sb("rec", [128, 4], F32, sa)
                ot = [sb("ot%d" % i, [128, 512], F32, sa) for i in range(2)]
                ats = [sb("ats%d" % i, [128, 512], BF16, sa) for i in range(2)]
                ps_s = [sa.enter_context(nc.psum_tensor("psS%d" % i, [128, 512], F32)) for i in range(2)]
                ps_o = [sa.enter_context(nc.psum_tensor("psO%d" % i, [128, 512], F32)) for i in range(6)]
                orot = Rot([(p, "psO%d" % i) for i, p in enumerate(ps_o)])
                srot = Rot([(p, "psS%d" % i) for i, p in enumerate(ps_s)])
                ptrot = Rot([(p, "pt%d" % i) for i, p in enumerate(pt)])
                pdrot = Rot([(p, "pd%d" % i) for i, p in enumerate(pd)])

                P.dma("sp", lambda e: e.dma_start(out=gbias[:], in_=gbias_d), writes=["gbias"])
                for i in range(2):
                    P.op("dve", lambda e, i=i: e.memset(Vt[i][:, :, 128:130], 1.0), writes=[("Vone", i)])

                def load_head(h):
                    b = h % 2
                    P.dma("sp", lambda e: e.dma_start(out=kT[b][:], in_=kT_scr[h:h + 1].rearrange("h g d t -> d (h g) t")),
                          writes=[("kT", b)])
                    P.dma("sp", lambda e: e.dma_start(
                        out=Vt[b][:, :, 0:128],
                        in_=v_scr[:, h * 128:(h + 1) * 128].rearrange("(t p) d -> p t d", p=128)),
                        writes=[("Vt", b)])
                    P.dma("sp", lambda e: e.dma_start(
                        out=qT[b][:].rearrange("p (g t) -> p g t", g=8),
                        in_=qT_scr[h:h + 1].rearrange("h g d t -> d (h g) t")), writes=[("qT", b)])

                obank_i = [0]

                def att_group(h, g, b, kTb, Vb, qTb, selb):
                    kkT, kV, kq, ksel = ("kT", b), ("Vt", b), ("qT", b), ("sel", b)
                    par = (h * 4 + g) % 2
                    oa = oacc[par]
                    koa4 = [("oacc", par, qi) for qi in range(4)]
                    emit_cast([P.op("dve", lambda e: e.memset(oa[:], 0.0), writes=koa4)])
                    qg = qTb[:, g * 512:(g + 1) * 512]
                    nA = NPAST + 2 * g
                    nB = nA + 1

                    def scores(n, ncols, q0):
                        res = []
                        for kt in range(2):
                            ps, kps = srot.next()
                            P.op("pe", lambda e, ps=ps, kt=kt: e.matmul(
                                ps[:, 0:ncols], kTb[:, n, kt * 128:(kt + 1) * 128], qg[:, q0:q0 + ncols],
                                start=True, stop=True), reads=[kkT, kq], writes=[kps])
                            p_, kp_ = ptrot.next()
                            P.op("act", lambda e, ps=ps, p_=p_: e.activation(
                                out=p_[:, 0:ncols], in_=ps[:, 0:ncols], func=AF.Exp, scale=SCALE),
                                reads=[kps], writes=[kp_])
                            res.append((p_, kp_))
                        return res

                    def pv_sel(n, pts, qis, col0):
                        for qi in qis:
                            bank, kbank = orot.next()
                            for kt in range(2):
                                p_, kp_ = pts[kt]
                                c = (qi - col0) * 128
                                P.op("pe", lambda e, bank=bank, p_=p_, kt=kt, c=c: e.matmul(
                                    bank[:, 0:129], p_[:, c:c + 128], Vb[:, 2 * n + kt, 0:129],
                                    start=(kt == 0), stop=(kt == 1)),
                                    reads=[kp_, kV, ("Vone", b)], writes=[kbank])
                            P.op("dve", lambda e, bank=bank, qi=qi: e.scalar_tensor_tensor(
                                out=oa[:, qi, :], in0=bank[:, 0:129], scalar=selb[:, 4 * g + qi, n:n + 1],
                                in1=oa[:, qi, :], op0=ALU.mult, op1=ALU.add),
                                reads=[kbank, ksel, koa4[qi]], writes=[koa4[qi]])

                    def pv_own(n, pts, qi, col0, two):
                        bank, kbank = orot.next()
                        c = (qi - col0) * 128
                        lst = []
                        if two:
                            lst.append((pts[0][0][:, c:c + 128], pts[0][1], 0))
                            dsrc, kd, dkt = pts[1][0], pts[1][1], 1
                        else:
                            dsrc, kd, dkt = pts[0][0], pts[0][1], 0
                        pdt, kpd = pdrot.next()
                        P.op("dve", lambda e: e.tensor_tensor(
                            out=pdt[:], in0=dsrc[:, c:c + 128], in1=trib[:], op=ALU.mult),
                            reads=[kd, "trib"], writes=[kpd])
                        lst.append((pdt[:], kpd, dkt))
                        nl = len(lst)
                        for i, (lh, klh, kt) in enumerate(lst):
                            P.op("pe", lambda e, lh=lh, kt=kt, i=i: e.matmul(
                                bank[:, 0:129], lh, Vb[:, 2 * n + kt, 0:129],
                                start=(i == 0), stop=(i == nl - 1)),
                                reads=[klh, kV, ("Vone", b)], writes=[kbank])
                        P.op("dve", lambda e: e.tensor_tensor(
                            out=oa[:, qi, :], in0=bank[:, 0:129], in1=oa[:, qi, :], op=ALU.add),
                            reads=[kbank, koa4[qi]], writes=[koa4[qi]])

                    prev = None
                    for n in range(nA):
                        cur = scores(n, 512, 0)
                        if prev is not None:
                            pv_sel(n - 1, prev, range(4), 0)
                        prev = cur
                    curA = scores(nA, 512, 0)
                    pv_sel(nA - 1, prev, range(4), 0)
                    curB = scores(nB, 256, 256)
                    pv_sel(nA, curA, (2, 3), 0)
                    pv_own(nA, curA, 0, 0, False)
                    pv_own(nA, curA, 1, 0, True)
                    pv_own(nB, curB, 2, 2, False)
                    pv_own(nB, curB, 3, 2, True)
                    P.op("dve", lambda e: e.reciprocal(out=rec[:, :], in_=oa[:, :, 128]),
                         reads=koa4, writes=["rec"])
                    o_ = ot[par]
                    ko = ("ot", par)
                    for qi in range(4):
                        P.op("dve", lambda e, qi=qi: e.tensor_scalar(
                            out=o_[:, qi * 128:(qi + 1) * 128], in0=oa[:, qi, 0:128], scalar1=rec[:, qi:qi + 1],
                            scalar2=None, op0=ALU.mult), reads=[koa4[qi], "rec"], writes=[ko])
                    tb, ktb = orot.next()
                    for qi in range(4):
                        P.op("pe", lambda e, qi=qi: e.transpose(
                            out=tb[:, qi * 128:(qi + 1) * 128], in_=o_[:, qi * 128:(qi + 1) * 128],
                            identity=idf[:]), reads=[ko, "idf"], writes=[ktb])
                    a_ = ats[par]
                    ka = ("ats", par)
                    copy_op(a_[:, :], tb[:, 0:512], [ktb], [ka])
                    P.dma("sp", lambda e: e.dma_start(
                        out=at_scr[h * 128:(h + 1) * 128, g * 512:(g + 1) * 512], in_=a_[:]), reads=[ka])

                def att_head(h):
                    b = h % 2
                    kTb, Vb, qTb, selb = kT[b], Vt[b], qT[b], sel[b]
                    kkT, kq, ksel = ("kT", b), ("qT", b), ("sel", b)
                    P.op("dve", lambda e: e.tensor_reduce(out=kms[:], in_=kTb[:], axis=AX.X, op=ALU.add),
                         reads=[kkT], writes=["kms"])
                    P.op("dve", lambda e: e.tensor_scalar(out=kmb[:], in0=kms[:], scalar1=1.0 / 256.0, scalar2=None,
                                                          op0=ALU.mult), reads=["kms"], writes=["kmb"])
                    ps_g, kpg = orot.next()
                    for qt in range(16):
                        P.op("pe", lambda e, qt=qt: e.matmul(
                            ps_g[:, qt * 32:(qt + 1) * 32], qTb[:, qt * 128:(qt + 1) * 128], kmb[:, :],
                            start=True, stop=True), reads=[kq, "kmb"], writes=[kpg])
                    P.op("dve", lambda e: e.tensor_tensor(out=gm[:].rearrange("p a b -> p (a b)"), in0=ps_g[:, :],
                                                          in1=gbias[:].rearrange("p a b -> p (a b)"), op=ALU.add),
                         reads=[kpg, "gbias"], writes=["gm"])
                    for qt in range(16):
                        P.op("dve", lambda e, qt=qt: e.max(out=m8[:, qt, :], in_=gm[:, qt, :]),
                             reads=["gm"], writes=["m8"])
                    P.op("dve", lambda e: e.tensor_scalar(out=thr[:], in0=m8[:, :, 2], scalar1=-1.0e29, scalar2=None,
                                                          op0=ALU.max), reads=["m8"], writes=["thr"])
                    for qt in range(16):
                        P.op("dve", lambda e, qt=qt: e.tensor_scalar(
                            out=selb[:, qt, :], in0=gm[:, qt, :], scalar1=thr[:, qt:qt + 1], scalar2=None,
                            op0=ALU.is_ge), reads=["gm", "thr"], writes=[ksel])
                    for g in range(4):
                        att_group(h, g, b, kTb, Vb, qTb, selb)

                load_head(0)
                for h in range(H):
                    if h + 1 < H:
                        load_head(h + 1)
                    att_head(h)
                while cast_i[0] < len(CASTS):
                    emit_cast([])
                P.barrier()

        if "B" in phases:
            with contextlib.ExitStack() as sa:
                xT = sb("xT", [128, KC, 512], F32, sa)
                hTb = sb("hTB", [128, KC, 512], BF16, sa)
                R = sb("R", [128, 24704], BF16, sa)
                pan = [sb("pan%d" % i, [128, KC * 512], BF16, sa) for i in range(4)]
                xsb = sb("xsB", [128, D], F32, sa)
                sqb = [sb("sqb%d" % i, [128, 2, 512], BF16, sa) for i in range(2)]
                rstd = sb("rstd", [128, 512], F32, sa)
                tmpf = [sb("tmpf%d" % i, [128, 512], F32, sa) for i in range(4)]
                px = [sb("px%d" % i, [128, 512], BF16, sa) for i in range(4)]
                hq = sb("hq", [128, XH, 512], BF16, sa)
                xo = sb("xo", [128, XH, 512], BF16, sa)
                cxh = sb("cxh", [128, 8, 2], F32, sa)
                uh = sb("uh", [128, 8, 2], BF16, sa)
                psf = [sa.enter_context(nc.psum_tensor("psfB%d" % i, [128, 512], F32)) for i in range(8)]
                pfrot = Rot([(p, "pfB%d" % i) for i, p in enumerate(psf)])
                panrot = Rot([(p, "pan%d" % i) for i, p in enumerate(pan)])
                tfrot = Rot([(p, "tmpf%d" % i) for i, p in enumerate(tmpf)])
                pxrot = Rot([(p, "px%d" % i) for i, p in enumerate(px)])

                cxs = R[:, 0:4096].rearrange("p (m t) -> p m t", m=8)
                ycT = R[:, 4096:8192].rearrange("p (m t) -> p m t", m=8)
                atT = R[:, 8192:12288].rearrange("p (m t) -> p m t", m=8)
                uc = R[:, 12288:16400].rearrange("p (m t) -> p m t", m=8)
                mg = R[:, 16512:24704].rearrange("p (m t) -> p m t", m=16)
                uT = R[:, 0:16384].rearrange("p (f t) -> p f t", f=32)
                RK = ["Rcxs", "Ryc", "Rat", "Ru"]

                def norm_fm(gi, out3, kout):
                    ps, kps = pfrot.next()
                    for q in range(8):
                        s = sqb[q % 2]
                        ks = ("sqb", q % 2)
                        P.op("act", lambda e, s=s, q=q: e.activation(out=s[:, :, :], in_=xT[:, 2 * q:2 * q + 2, :],
                                                                     func=AF.Square), reads=[("xT", 2 * q), ("xT", 2 * q + 1)], writes=[ks])
                        for j in range(2):
                            P.op("pe", lambda e, s=s, j=j, q=q, ps=ps: e.matmul(
                                ps[:, :], onesb[:, :], s[:, j, :], start=(q == 0 and j == 0),
                                stop=(q == 7 and j == 1)), reads=[ks, "onesb"], writes=[kps])
                    P.op("dve", lambda e, ps=ps: e.tensor_scalar(out=rstd[:], in0=ps[:, :], scalar1=1.0 / D, scalar2=EPS,
                                                                 op0=ALU.mult, op1=ALU.add), reads=[kps], writes=["rstd"])
                    P.op("act", lambda e: e.activation(out=rstd[:], in_=rstd[:], func=AF.Sqrt),
                         reads=["rstd"], writes=["rstd"])
                    P.op("dve", lambda e: e.reciprocal(out=rstd[:], in_=rstd[:]), reads=["rstd"], writes=["rstd"])
                    for kc in range(KC):
                        P.op("dve", lambda e, kc=kc: e.scalar_tensor_tensor(
                            out=out3[:, kc, :], in0=xT[:, kc, :], scalar=gT[:, gi, kc:kc + 1], in1=rstd[:],
                            op0=ALU.mult, op1=ALU.mult), reads=[("xT", kc), "gT", "rstd"], writes=[(kout, kc)])

                slot_i = [0]

                def get_panel(wap, r0, kcn, c0, ncol=512):
                    slot = slot_i[0] % len(PANELS)
                    slot_i[0] += 1
                    spec = PANELS[slot]
                    assert len(spec) == 1 and spec[0][1:5] == (r0, kcn, c0, ncol) and spec[0][0] is wap, (slot, spec)
                    p_, kp_ = panrot.next()
                    n = kcn * ncol
                    P.dma("pool", lambda e: e.dma_start(out=p_[:, 0:n], in_=wscr[slot, :, 0:n]), writes=[kp_])
                    return p_[:, 0:n].rearrange("p (k n) -> p k n", k=kcn), kp_

                def lin_fm(pv, kpv, kcn, m, rhs_fn, rkeys, ps, kps, ncols=512, first=True, last=True):
                    for kc in range(kcn):
                        P.op("pe", lambda e, kc=kc: e.matmul(
                            ps[:, 0:ncols], pv[:, kc, m * 128:(m + 1) * 128], rhs_fn(kc),
                            start=(first and kc == 0), stop=(last and kc == kcn - 1)),
                            reads=[kpv] + (rkeys(kc) if callable(rkeys) else rkeys), writes=[kps])

                for g in range(4):
                    for ti in range(4):
                        t = 48 + 4 * g + ti
                        P.dma("sp", lambda e, t=t: e.dma_start(out=xsb[:], in_=xc[t * 128:(t + 1) * 128, :]),
                              writes=["xsB"])
                        for q in range(4):
                            ps, kps = pfrot.next()
                            for k in range(4):
                                P.op("pe", lambda e, ps=ps, k=k, q=q: e.transpose(
                                    out=ps[:, k * 128:(k + 1) * 128], in_=xsb[:, (4 * q + k) * 128:(4 * q + k + 1) * 128],
                                    identity=idf[:]), reads=["xsB", "idf"], writes=[kps])
                            copy_op(xT[:, 4 * q:4 * q + 4, ti * 128:(ti + 1) * 128],
                                    ps[:, :].rearrange("p (k t) -> p k t", k=4), [kps], [("xT", 4 * q + k) for k in range(4)])
                    norm_fm(0, hTb, "hT")
                    if g > 0:
                        P.op("dve", lambda e: e.tensor_copy(out=uc[:, :, 0:2], in_=uh[:, :, :]),
                             reads=["uh"], writes=["Ru"])
                    for pg in range(2):
                        pv, kpv = get_panel(w_in, 0, KC, 3072 + pg * 512)
                        for mm in range(4):
                            m = pg * 4 + mm
                            ps, kps = pfrot.next()
                            lin_fm(pv, kpv, KC, mm, lambda kc: hTb[:, kc, :], (lambda kc: [("hT", kc)]), ps, kps)
                            copy_op(cxs[:, m, :], ps[:, :], [kps], ["Rcxs"])
                            if g == 0:
                                ps2, kps2 = pfrot.next()
                                lin_fm(pv, kpv, KC, mm, lambda kc: hTh[:, kc, :], ["hTh"], ps2, kps2, ncols=2)
                                copy_op(cxh[:, m, :], ps2[:, 0:2], [kps2], ["cxh"])
                    for pg in range(2):
                        pv, kpv = get_panel(w_in, 0, KC, 5120 + pg * 512)
                        for mm in range(4):
                            m = pg * 4 + mm
                            ps, kps = pfrot.next()
                            lin_fm(pv, kpv, KC, mm, lambda kc: hTb[:, kc, :], (lambda kc: [("hT", kc)]), ps, kps)
                            P.op("dve", lambda e, ps=ps, m=m: e.tensor_tensor(
                                out=uc[:, m, 2:514], in0=ps[:, :], in1=cxs[:, m, :], op=ALU.mult),
                                reads=[kps, "Rcxs"], writes=["Ru"])
                            if g == 0:
                                ps2, kps2 = pfrot.next()
                                lin_fm(pv, kpv, KC, mm, lambda kc: hTh[:, kc, :], ["hTh"], ps2, kps2, ncols=2)
                                P.op("dve", lambda e, ps2=ps2, m=m: e.tensor_tensor(
                                    out=uc[:, m, 0:2], in0=ps2[:, 0:2], in1=cxh[:, m, :], op=ALU.mult),
                                    reads=[kps2, "cxh"], writes=["Ru"])
                    P.op("dve", lambda e: e.tensor_copy(out=uh[:, :, :], in_=uc[:, :, 512:514]),
                         reads=["Ru"], writes=["uh"])
                    for pg in range(2):
                        pv, kpv = get_panel(w_in, 0, KC, 4096 + pg * 512)
                        for mm in range(4):
                            m = pg * 4 + mm
                            ps, kps = pfrot.next()
                            lin_fm(pv, kpv, KC, mm, lambda kc: hTb[:, kc, :], (lambda kc: [("hT", kc)]), ps, kps)
                            y, ky = tfrot.next()
                            P.op("dve", lambda e, y=y, m=m: e.tensor_scalar(
                                out=y[:], in0=uc[:, m, 2:514], scalar1=cwT[:, m, 2:3], scalar2=None, op0=ALU.mult),
                                reads=["Ru", "cwT"], writes=[ky])
                            P.op("dve", lambda e, y=y, m=m: e.scalar_tensor_tensor(
                                out=y[:], in0=uc[:, m, 1:513], scalar=cwT[:, m, 1:2], in1=y[:],
                                op0=ALU.mult, op1=ALU.add), reads=["Ru", "cwT", ky], writes=[ky])
                            P.op("dve", lambda e, y=y, m=m: e.scalar_tensor_tensor(
                                out=y[:], in0=uc[:, m, 0:512], scalar=cwT[:, m, 0:1], in1=y[:],
                                op0=ALU.mult, op1=ALU.add), reads=["Ru", "cwT", ky], writes=[ky])
                            P.op("dve", lambda e, y=y, m=m, ps=ps: e.tensor_tensor(
                                out=ycT[:, m, :], in0=ps[:, :], in1=y[:], op=ALU.mult),
                                reads=[kps, ky], writes=["Ryc"])
                    P.dma("sp", lambda e, g=g: e.dma_start(
                        out=atT, in_=at_scr[:, g * 512:(g + 1) * 512].rearrange("(m p) t -> p m t", p=128)),
                        writes=["Rat"])
                    for cgp in range(4):
                        pga, kga = get_panel(w_in, 0, KC, 6144 + cgp * 512)
                        pgc, kgc = get_panel(w_in, 0, KC, 8192 + cgp * 512)
                        slot = slot_i[0] % len(PANELS)
                        slot_i[0] += 1
                        assert len(PANELS[slot]) == 2
                        p_, kp_ = panrot.next()
                        pao = p_[:, 0:4096].rearrange("p (k n) -> p k n", k=8)
                        pco = p_[:, 4096:8192].rearrange("p (k n) -> p k n", k=8)
                        P.dma("pool", lambda e, p_=p_, slot=slot: e.dma_start(out=p_[:, :], in_=wscr[slot, :, :]),
                              writes=[kp_])
                        for mm in range(4):
                            m = cgp * 4 + mm
                            psa, ka = pfrot.next()
                            lin_fm(pga, kga, KC, mm, lambda kc: hTb[:, kc, :], (lambda kc: [("hT", kc)]), psa, ka)
                            psc, kc_ = pfrot.next()
                            lin_fm(pgc, kgc, KC, mm, lambda kc: hTb[:, kc, :], (lambda kc: [("hT", kc)]), psc, kc_)
                            psy, ky_ = pfrot.next()
                            lin_fm(pao, kp_, 8, mm, lambda kc: atT[:, kc, :], ["Rat"], psy, ky_)
                            psz, kz_ = pfrot.next()
                            lin_fm(pco, kp_, 8, mm, lambda kc: ycT[:, kc, :], ["Ryc"], psz, kz_)
                            sa_, ksa = tfrot.next()
                            sc_, ksc = tfrot.next()
                            P.op("act", lambda e, sa_=sa_, psa=psa: e.activation(out=sa_[:], in_=psa[:, :], func=AF.Sigmoid),
                                 reads=[ka], writes=[ksa])
                            P.op("act", lambda e, sc_=sc_, psc=psc: e.activation(out=sc_[:], in_=psc[:, :], func=AF.Sigmoid),
                                 reads=[kc_], writes=[ksc])
                            P.op("dve", lambda e, sa_=sa_, psy=psy: e.tensor_tensor(out=sa_[:], in0=psy[:, :], in1=sa_[:], op=ALU.mult),
                                 reads=[ky_, ksa], writes=[ksa])
                            P.op("dve", lambda e, sc_=sc_, psz=psz: e.tensor_tensor(out=sc_[:], in0=psz[:, :], in1=sc_[:], op=ALU.mult),
                                 reads=[kz_, ksc], writes=[ksc])
                            P.op("dve", lambda e, sa_=sa_, sc_=sc_, m=m: e.tensor_tensor(out=mg[:, m, :], in0=sa_[:], in1=sc_[:], op=ALU.add),
                                 reads=[ksa, ksc], writes=["Rmg"])

                    def resid_layer(wap, kcn, rhs3, rkeys):
                        for cgp in range(4):
                            pv, kpv = get_panel(wap, 0, kcn, cgp * 512)
                            for mm in range(4):
                                m = cgp * 4 + mm
                                ps, kps = pfrot.next()
                                lin_fm(pv, kpv, kcn, mm, lambda kc: rhs3[:, kc, :], rkeys, ps, kps)
                                P.op("dve", lambda e, ps=ps, m=m: e.tensor_tensor(
                                    out=xT[:, m, :], in0=ps[:, :], in1=xT[:, m, :], op=ALU.add),
                                    reads=[kps, ("xT", m)], writes=[("xT", m)])

                    resid_layer(w_mix, KC, mg, ["Rmg"])
                    norm_fm(1, hTb, "hT")
                    pv, kpv = get_panel(wq_x, 0, KC, 0)
                    for hd in range(XH):
                        ps, kps = pfrot.next()
                        lin_fm(pv, kpv, KC, hd, lambda kc: hTb[:, kc, :], (lambda kc: [("hT", kc)]), ps, kps)
                        copy_op(hq[:, hd, :], ps[:, :], [kps], ["hq"])
                    for hd in range(XH):
                        pxs = []
                        for mt in range(2):
                            ps, kps = pfrot.next()
                            P.op("pe", lambda e, ps=ps, mt=mt, hd=hd: e.matmul(
                                ps[:, :], mkT[:, hd, mt * 128:(mt + 1) * 128], hq[:, hd, :], start=True, stop=True),
                                reads=["mkT", "hq"], writes=[kps])
                            p_, kp_ = pxrot.next()
                            P.op("act", lambda e, ps=ps, p_=p_: e.activation(out=p_[:], in_=ps[:, :], func=AF.Exp, scale=SCALE),
                                 reads=[kps], writes=[kp_])
                            pxs.append((p_, kp_))
                        pso, kpo = pfrot.next()
                        psz, kpz = pfrot.next()
                        for mt in range(2):
                            p_, kp_ = pxs[mt]
                            P.op("pe", lambda e, pso=pso, mt=mt, hd=hd, p_=p_: e.matmul(
                                pso[:, :], mv[:, mt, hd * 128:(hd + 1) * 128], p_[:], start=(mt == 0), stop=(mt == 1)),
                                reads=["mv", kp_], writes=[kpo])
                        for mt in range(2):
                            p_, kp_ = pxs[mt]
                            P.op("pe", lambda e, psz=psz, mt=mt, p_=p_: e.matmul(
                                psz[:, :], onesb[:, :], p_[:], start=(mt == 0), stop=(mt == 1)),
                                reads=["onesb", kp_], writes=[kpz])
                        rz, krz = tfrot.next()
                        P.op("dve", lambda e, rz=rz, psz=psz: e.reciprocal(out=rz[:], in_=psz[:, :]), reads=[kpz], writes=[krz])
                        P.op("dve", lambda e, rz=rz, pso=pso, hd=hd: e.tensor_tensor(
                            out=xo[:, hd, :], in0=pso[:, :], in1=rz[:], op=ALU.mult), reads=[kpo, krz], writes=["xo"])
                    resid_layer(wo_x, XH, xo, ["xo"])
                    norm_fm(2, hTb, "hT")
                    for half in range(2):
                        for pg in range(8):
                            pv, kpv = get_panel(w_up, 0, KC, half * 4096 + pg * 512)
                            for mm in range(4):
                                f = pg * 4 + mm
                                ps, kps = pfrot.next()
                                lin_fm(pv, kpv, KC, mm, lambda kc: hTb[:, kc, :], (lambda kc: [("hT", kc)]), ps, kps)
                                sq, ksq = tfrot.next()
                                P.op("act", lambda e, sq=sq, ps=ps: e.activation(out=sq[:], in_=ps[:, :], func=AF.Square),
                                     reads=[kps], writes=[ksq])
                                P.op("dve", lambda e, sq=sq, ps=ps, f=f: e.scalar_tensor_tensor(
                                    out=uT[:, f, :], in0=ps[:, :], scalar=0.0, in1=sq[:], op0=ALU.is_gt, op1=ALU.mult),
                                    reads=[kps, ksq], writes=[RK[f // 8]])
                        for cgp in range(4):
                            banks = [pfrot.next() for _ in range(4)]
                            for qd in range(2):
                                pv, kpv = get_panel(w_down, half * 4096 + qd * 2048, KC, cgp * 512)
                                for mm in range(4):
                                    ps, kps = banks[mm]
                                    lin_fm(pv, kpv, KC, mm, lambda kc, qd=qd: uT[:, qd * 16 + kc, :],
                                           [RK[2 * qd], RK[2 * qd + 1]], ps, kps, first=(qd == 0), last=(qd == 1))
                            for mm in range(4):
                                m = cgp * 4 + mm
                                ps, kps = banks[mm]
                                P.op("dve", lambda e, ps=ps, m=m: e.tensor_tensor(
                                    out=xT[:, m, :], in0=ps[:, :], in1=xT[:, m, :], op=ALU.add),
                                    reads=[kps, ("xT", m)], writes=[("xT", m)])
                    norm_fm(3, xT, "xT")
                    for ti in range(4):
                        for q in range(4):
                            ps, kps = pfrot.next()
                            for k in range(4):
                                P.op("pe", lambda e, ps=ps, k=k, q=q, ti=ti: e.transpose(
                                    out=ps[:, k * 128:(k + 1) * 128], in_=xT[:, 4 * q + k, ti * 128:(ti + 1) * 128],
                                    identity=idf[:]), reads=[("xT", 4 * q + k), "idf"], writes=[kps])
                            copy_op(xsb[:, q * 512:(q + 1) * 512], ps[:, :], [kps], ["xsB"])
                        r0 = g * 512 + ti * 128
                        P.dma("sp", lambda e, r0=r0: e.dma_start(out=out_d[r0:r0 + 128, :], in_=xsb[:]), reads=["xsB"])

        P.finalize()
        esems = {en: es.enter_context(nc.semaphore("es_" + en)) for en in P.ENGS}
        dsems = {k: es.enter_context(nc.semaphore("ds_%s_%d" % k)) for k in P.dma_uses}
        block = es.enter_context(nc.Block())

        @block.sync
        def _(e):
            P.emit_engine("sp", e, esems, dsems)

        @block.scalar
        def _(e):
            P.emit_engine("act", e, esems, dsems)

        @block.vector
        def _(e):
            P.emit_engine("dve", e, esems, dsems)

        @block.gpsimd
        def _(e):
            P.emit_engine("pool", e, esems, dsems)

        @block.tensor
        def _(e):
            P.emit_engine("pe", e, esems, dsems)
    return nc


def _consts(j):
    half = 16
    inv_freq = (500000.0 ** (-np.arange(half, dtype=np.float32) / half)).astype(np.float32)
    pos = (np.arange(NCTX) + (j - 3) * NOWN).astype(np.float32)
    ang = pos[:, None] * inv_freq[None, :]
    cos, sin = np.cos(ang).astype(np.float32), np.sin(ang).astype(np.float32)
    rope = np.concatenate([np.tile(cos, (1, 4)), np.tile(sin, (1, 4))], axis=1).astype(np.float32)
    gb = np.full((16, NBLK), NEG, np.float32)
    first_valid = NPAST - 8 * j
    for qt in range(16):
        ob = NPAST + qt // 2
        gb[qt, first_valid:ob] = 0.0
    gbias = np.ascontiguousarray(np.broadcast_to(gb[None], (128, 16, NBLK))).astype(np.float32)
    return rope, gbias


_NC_CACHE = {}


def make_in_maps(inputs):
    f = lambda k: np.ascontiguousarray(np.asarray(inputs[k], dtype=np.float32))
    x, mem = f("x"), f("mem")
    shared = {
        "w_in": f("w_in")[0], "w_attn_out": f("w_attn_out")[0], "w_conv_out": f("w_conv_out")[0],
        "w_mix_out": f("w_mix_out")[0], "wq_x": f("wq_x")[0], "wkv_x": f("wkv_x")[0], "wo_x": f("wo_x")[0],
        "w_up": f("w_up")[0], "w_down": f("w_down")[0],
    }
    gbc = np.stack([np.broadcast_to(f("norm_mix")[0][None], (128, D)),
                    np.broadcast_to(f("norm_mem")[0][None], (128, D))]).astype(np.float32)
    gT = np.stack([f("norm_mix")[0], f("norm_xattn")[0], f("norm_mlp")[0], f("norm_final")], 0)
    gT = np.ascontiguousarray(gT.reshape(4, KC, 128).transpose(2, 0, 1))
    cwT = np.ascontiguousarray(f("conv_w")[0].reshape(3, 8, 128).transpose(2, 1, 0))
    shared.update({"gbc": np.ascontiguousarray(gbc), "gT": gT, "cwT": cwT,
                   "ident": np.eye(128, dtype=np.float32),
                   "tri": np.triu(np.ones((128, 128), np.float32))})
    in_maps = []
    for c in range(8):
        b, j = c // 4, c % 4
        xcw = np.zeros((NCTX, D), np.float32)
        n = (j + 1) * NOWN
        xcw[NCTX - n:] = x[b, :n]
        rope, gbias = _consts(j)
        m = dict(shared)
        m.update({"xc": xcw, "mem": mem[b], "rope": rope, "gbias": gbias})
        in_maps.append(m)
    return in_maps


def kernel(**inputs):
    if "nc" not in _NC_CACHE:
        _NC_CACHE["nc"] = build()
    nc = _NC_CACHE["nc"]
    in_maps = make_in_maps(inputs)
    res = run_bass_kernel_spmd(nc, in_maps, core_ids=list(range(8)))
    out = np.zeros((2, 8192, D), np.float32)
    for c in range(8):
        b, j = c // 4, c % 4
        out[b, j * NOWN:(j + 1) * NOWN] = np.asarray(res.results[c]["out"], dtype=np.float32)
    return out
```

```python
import contextlib

import numpy as np

import concourse.bass as bass
import concourse.mybir as mybir
from concourse.bass_utils import run_bass_kernel_spmd

F32 = mybir.dt.float32
BF16 = mybir.dt.bfloat16
ALU = mybir.AluOpType
AF = mybir.ActivationFunctionType
AX = mybir.AxisListType

D = 2048
KC = 16
NCTX = 8192
NOWN = 2048
H = 8
DH = 128
NBLK = 32
NPAST = 24
XH = 4
DFF = 8192
EPS = 1e-6
NEG = -1.0e30
SCALE = DH ** -0.5


class Ins:
    __slots__ = ("eng", "emit", "deps", "mark", "cnt", "is_dma", "dsem", "dval", "gid")

    def __init__(self, eng, emit, is_dma=False):
        self.eng = eng
        self.emit = emit
        self.deps = []
        self.mark = False
        self.cnt = 0
        self.is_dma = is_dma
        self.dsem = None
        self.dval = 0
        self.gid = 0


class Planner:
    ENGS = ("pe", "act", "dve", "pool", "sp")

    def __init__(self, n_dma_sems=12):
        self.streams = {e: [] for e in self.ENGS}
        self.reg = {}
        self.n_dma_sems = n_dma_sems
        self.dma_rr = {e: 0 for e in self.ENGS}
        self.dma_last = {}
        self.dma_uses = {}
        self.all = []
        self.open_dmas = []

    def _add(self, ins, reads, writes):
        deps = set()
        for k in reads:
            st = self.reg.get(k)
            if st is not None and st[0] is not None:
                deps.add(st[0])
            if st is not None and isinstance(k, str) and k.startswith("p"):
                deps.update(v for en, v in st[1].items() if en != ins.eng)
        for k in writes:
            st = self.reg.get(k)
            if st is not None:
                if st[0] is not None:
                    deps.add(st[0])
                deps.update(st[1].values())
                deps.update(st[2])
        if ins.eng == "pe" and not ins.is_dma:
            deps = {d for d in deps if not (d.eng == "pe" and not d.is_dma)}
        deps.discard(ins)
        ins.deps = list(deps)
        for k in reads:
            st = self.reg.setdefault(k, [None, {}, []])
            if ins.is_dma:
                st[2].append(ins)
            else:
                st[1][ins.eng] = ins
        for k in writes:
            self.reg[k] = [ins, {}, []]
        ins.gid = len(self.all)
        self.all.append(ins)
        self.streams[ins.eng].append(ins)
        return ins

    def op(self, eng, emit, reads=(), writes=()):
        return self._add(Ins(eng, emit), reads, writes)

    def dma(self, eng, emit, reads=(), writes=(), after=()):
        ins = Ins(eng, emit, is_dma=True)
        slot = self.dma_rr[eng] % self.n_dma_sems
        self.dma_rr[eng] += 1
        key = (eng, slot)
        n = self.dma_uses.get(key, 0) + 1
        self.dma_uses[key] = n
        ins.dsem = key
        ins.dval = 16 * n
        self._add(ins, reads, writes)
        ins.deps.extend(after)
        prev = self.dma_last.get(key)
        if prev is not None:
            ins.deps.append(prev)
        self.dma_last[key] = ins
        self.open_dmas.append(ins)
        return ins

    def barrier(self):
        deps = []
        for st in self.streams.values():
            for ins in reversed(st):
                if not ins.is_dma and ins.emit is not None:
                    deps.append(ins)
                    break
        deps = deps + list(self.open_dmas)
        self.open_dmas = []
        for e in self.ENGS:
            ins = Ins(e, None)
            ins.deps = [d for d in deps]
            ins.gid = len(self.all)
            self.all.append(ins)
            self.streams[e].append(ins)
        self.reg = {}

    def finalize(self):
        for ins in self.all:
            for d in ins.deps:
                d.mark = True
        for e in self.ENGS:
            c = 0
            for ins in self.streams[e]:
                if ins.mark and not ins.is_dma:
                    c += 1
                    ins.cnt = c

    def emit_engine(self, eng, e, esems, dsems):
        known = {}
        stream = self.streams[eng]
        look = 0

        def kv(d):
            if d.is_dma:
                return ("d",) + d.dsem, d.dval
            return ("e", d.eng), d.cnt

        for pos, ins in enumerate(stream):
            need = {}
            for d in ins.deps:
                key, val = kv(d)
                if need.get(key, 0) < val:
                    need[key] = val
            for key, val in need.items():
                if known.get(key, 0) >= val:
                    continue
                for nxt in stream[pos + 1:pos + 1 + look]:
                    for d in nxt.deps:
                        if d.gid < ins.gid and not d.is_dma:
                            k2, v2 = kv(d)
                            if k2 == key and v2 > val:
                                val = v2
                known[key] = val
                sem = dsems[key[1:]] if key[0] == "d" else esems[key[1]]
                e.wait_ge(sem, val)
            if ins.emit is None:
                continue
            bi = ins.emit(e)
            if ins.is_dma:
                bi.then_inc(dsems[ins.dsem], 16)
            elif ins.mark:
                bi.then_inc(esems[eng], 1)
        for (de, slot), n in self.dma_uses.items():
            if de == eng:
                key = ("d", de, slot)
                if known.get(key, 0) < 16 * n:
                    e.wait_ge(dsems[(de, slot)], 16 * n)


class Rot:
    def __init__(self, items):
        self.items = items
        self.i = 0

    def next(self):
        it = self.items[self.i % len(self.items)]
        self.i += 1
        return it


def build(debug=False, phases="MATB"):
    nc = bass.Bass("TRN2", target_bir_lowering=False)

    def din(name, shape, dt=F32):
        return nc.dram_tensor(name, list(shape), dt, kind="ExternalInput").ap()

    xc = din("xc", [NCTX, D])
    mem = din("mem", [256, D])
    w_in = din("w_in", [D, 10240])
    w_ao = din("w_attn_out", [1024, D])
    w_co = din("w_conv_out", [1024, D])
    w_mix = din("w_mix_out", [D, D])
    wq_x = din("wq_x", [D, 512])
    wkv_x = din("wkv_x", [D, 1024])
    wo_x = din("wo_x", [512, D])
    w_up = din("w_up", [D, DFF])
    w_down = din("w_down", [DFF, D])
    gbc_d = din("gbc", [2, 128, D])
    gT_d = din("gT", [128, 4, KC])
    cwT_d = din("cwT", [128, 8, 3])
    rope_d = din("rope", [NCTX, 128])
    gbias_d = din("gbias", [128, 16, NBLK])
    ident_d = din("ident", [128, 128])
    tri_d = din("tri", [128, 128])

    skind = "ExternalOutput" if debug else "Internal"
    kT_scr = nc.dram_tensor("kT_scr", [H, 32, 128, 256], BF16, kind=skind).ap()
    v_scr = nc.dram_tensor("v_scr", [NCTX, 1024], BF16, kind=skind).ap()
    qT_scr = nc.dram_tensor("qT_scr", [H, 8, 128, 256], BF16, kind=skind).ap()
    at_scr = nc.dram_tensor("at_scr", [1024, NOWN], BF16, kind=skind).ap()
    out_d = nc.dram_tensor("out", [NOWN, D], F32, kind="ExternalOutput").ap()
    wscr = nc.dram_tensor("wscr", [59, 128, 8192], BF16, kind="Internal").ap()

    def panel_list():
        L = []
        for c0 in (3072, 5120, 4096):
            for pg in range(2):
                L.append([(w_in, 0, KC, c0 + pg * 512, 512, 0)])
        for cgp in range(4):
            L.append([(w_in, 0, KC, 6144 + cgp * 512, 512, 0)])
            L.append([(w_in, 0, KC, 8192 + cgp * 512, 512, 0)])
            L.append([(w_ao, 0, 8, cgp * 512, 512, 0), (w_co, 0, 8, cgp * 512, 512, 4096)])
        for cgp in range(4):
            L.append([(w_mix, 0, KC, cgp * 512, 512, 0)])
        L.append([(wq_x, 0, KC, 0, 512, 0)])
        for cgp in range(4):
            L.append([(wo_x, 0, XH, cgp * 512, 512, 0)])
        for half in range(2):
            for pg in range(8):
                L.append([(w_up, 0, KC, half * 4096 + pg * 512, 512, 0)])
            for cgp in range(4):
                for qd in range(2):
                    L.append([(w_down, half * 4096 + qd * 2048, KC, cgp * 512, 512, 0)])
        return L

    PANELS = panel_list()
    CASTS = [(slot,) + part for slot, parts in enumerate(PANELS) for part in parts]
    cast_i = [0]

    P = Planner()
    es = contextlib.ExitStack()
    with es:
        def sb(name, shape, dt, stack=es):
            return stack.enter_context(nc.sbuf_tensor("s_" + name, list(shape), dt))

        idf = sb("idf", [128, 128], F32)
        idb = sb("idb", [128, 128], BF16)
        trib = sb("trib", [128, 128], BF16)
        trif = sb("trif", [128, 128], F32)
        onesb = sb("onesb", [128, 128], BF16)
        gT = sb("gT", [128, 4, KC], F32)
        cwT = sb("cwT", [128, 8, 3], F32)
        mkT = sb("mkT", [128, XH, 256], BF16)
        mv = sb("mv", [128, 2, 512], BF16)
        hTh = sb("hTh", [128, KC, 2], BF16)
        stat = sb("stat", [128, 256], F32)
        junk = sb("junk", [128, D], BF16)

        P.dma("sp", lambda e: e.dma_start(out=idf[:], in_=ident_d), writes=["idf"])
        P.dma("sp", lambda e: e.dma_start(out=trif[:], in_=tri_d), writes=["trif"])
        P.dma("sp", lambda e: e.dma_start(out=gT[:], in_=gT_d), writes=["gT"])
        P.dma("sp", lambda e: e.dma_start(out=cwT[:], in_=cwT_d), writes=["cwT"])
        P.op("dve", lambda e: e.tensor_copy(out=idb[:], in_=idf[:]), reads=["idf"], writes=["idb"])
        P.op("dve", lambda e: e.tensor_copy(out=trib[:], in_=trif[:]), reads=["trif"], writes=["trib"])
        P.op("dve", lambda e: e.memset(onesb[:], 1.0), writes=["onesb"])
        P.op("dve", lambda e: e.memset(stat[:], 0.0), writes=["statinit"])

        stat_i = [0]
        cp_i = [0]

        def emit_cast(after):
            if cast_i[0] >= len(CASTS):
                return
            slot, wap, r0, kcn, c0, ncol, off = CASTS[cast_i[0]]
            cast_i[0] += 1
            src = wap[r0:r0 + kcn * 128, c0:c0 + ncol].rearrange("(kc p) n -> p kc n", p=128)
            dst = wscr[slot, :, off:off + kcn * ncol].rearrange("p (k n) -> p k n", k=kcn)
            P.dma("pool", lambda e: e.dma_start(out=dst, in_=src), writes=[("wscr", slot, off)], after=after)

        def copy_op(out, in_, reads, writes):
            cp_i[0] += 1
            if cp_i[0] % 2:
                P.op("act", lambda e: e.activation(out=out, in_=in_, func=AF.Copy), reads=reads, writes=writes)
            else:
                P.op("dve", lambda e: e.tensor_copy(out=out, in_=in_), reads=reads, writes=writes)

        def norm_tm(xs, kx, gbc, kg, hb, khb):
            i = stat_i[0]
            stat_i[0] += 1
            ks = ("stat", i)
            c0 = 3 * i
            P.op("act", lambda e: e.activation(out=junk[:], in_=xs, func=AF.Square,
                                               accum_out=stat[:, c0:c0 + 1]),
                 reads=[kx, "statinit"], writes=[ks])
            P.op("dve", lambda e: e.tensor_scalar(out=stat[:, c0 + 1:c0 + 2], in0=stat[:, c0:c0 + 1],
                                                  scalar1=1.0 / D, scalar2=EPS, op0=ALU.mult, op1=ALU.add),
                 reads=[ks], writes=[ks])
            P.op("act", lambda e: e.activation(out=stat[:, c0 + 2:c0 + 3], in_=stat[:, c0 + 1:c0 + 2],
                                               func=AF.Sqrt), reads=[ks], writes=[ks])
            P.op("dve", lambda e: e.reciprocal(out=stat[:, c0 + 2:c0 + 3], in_=stat[:, c0 + 2:c0 + 3]),
                 reads=[ks], writes=[ks])
            P.op("dve", lambda e: e.scalar_tensor_tensor(out=hb, in0=xs, scalar=stat[:, c0 + 2:c0 + 3],
                                                         in1=gbc, op0=ALU.mult, op1=ALU.mult),
                 reads=[kx, ks, kg], writes=[khb])

        def transpose_bf(src, ksrc, n, dst3, kdst, pbrot):
            for b0 in range(0, n, 8):
                nb = min(8, n - b0)
                pb, kpb = pbrot.next()
                for k in range(nb):
                    P.op("pe", lambda e, k=k, pb=pb, b0=b0: e.transpose(out=pb[:, k * 128:(k + 1) * 128],
                                                                  in_=src[:, (b0 + k) * 128:(b0 + k + 1) * 128],
                                                                  identity=idb[:]),
                         reads=[ksrc, "idb"], writes=[kpb])
                copy_op(dst3[:, b0:b0 + nb, :], pb[:, 0:nb * 128].rearrange("p (k q) -> p k q", k=nb),
                        [kpb], [kdst])

        def load_w(dst, wap, r0, kcn, c0, ncol, key):
            src = wap[r0:r0 + kcn * 128, c0:c0 + ncol].rearrange("(kc p) n -> p kc n", p=128)
            P.dma("pool", lambda e: e.dma_start(out=dst, in_=src), writes=[key])

        if "M" in phases or "A" in phases:
            with contextlib.ExitStack() as sa:
                wq = sb("wq", [128, KC, 1024], BF16, sa)
                wk = sb("wk", [128, KC, 1024], BF16, sa)
                wv = sb("wv", [128, KC, 1024], BF16, sa)
                gbc = sb("gbc", [128, 2, D], F32, sa)
                xs = [sb("xs%d" % i, [128, D], F32, sa) for i in range(2)]
                hb = [sb("hb%d" % i, [128, D], BF16, sa) for i in range(2)]
                hT = [sb("hT%d" % i, [128, KC, 128], BF16, sa) for i in range(2)]
                vb = [sb("vb%d" % i, [128, 1024], BF16, sa) for i in range(2)]
                kb = [sb("kb%d" % i, [128, 1024], BF16, sa) for i in range(2)]
                qb = [sb("qb%d" % i, [128, 1024], BF16, sa) for i in range(2)]
                kTs = [sb("kTs%d" % i, [128, H, 256], BF16, sa) for i in range(2)]
                qTs = [sb("qTs%d" % i, [128, H, 256], BF16, sa) for i in range(2)]
                ropet = [sb("ropet%d" % i, [128, 128], F32, sa) for i in range(2)]
                rtmp = [sb("rtmp%d" % i, [128, 256], F32, sa) for i in range(4)]
                psf = [sa.enter_context(nc.psum_tensor("psfA%d" % i, [128, 512], F32)) for i in range(6)]
                psb = [sa.enter_context(nc.psum_tensor("psbA%d" % i, [128, 1024], BF16)) for i in range(2)]
                pfrot = Rot([(p, "pfA%d" % i) for i, p in enumerate(psf)])
                pbrot = Rot([(p, "pbA%d" % i) for i, p in enumerate(psb)])
                rtrot = Rot([(p, "rtmp%d" % i) for i, p in enumerate(rtmp)])

                P.dma("sp", lambda e: e.dma_start(out=gbc[:], in_=gbc_d.rearrange("g p d -> p g d")),
                      writes=["gbc"])
                for cg in range(2):
                    load_w(wk[:, :, cg * 512:(cg + 1) * 512], wkv_x, 0, KC, cg * 512, 512, ("wk", cg))

                hmT = sb("hmT", [128, KC, 256], BF16, sa)
                import os
                KSTOP = int(os.environ.get("KSTOP", "99"))
                for mt in range(2):
                    s = mt % 2
                    P.dma("sp", lambda e, s=s, mt=mt: e.dma_start(out=xs[s][:], in_=mem[mt * 128:(mt + 1) * 128, :]),
                          writes=[("xs", s)])
                    if KSTOP >= 2:
                        norm_tm(xs[s][:], ("xs", s), gbc[:, 1, :], "gbc", hb[s][:], ("hb", s))
                    if KSTOP >= 3:
                        transpose_bf(hb[s], ("hb", s), KC, hmT[:, :, mt * 128:(mt + 1) * 128], ("hmT", mt), pbrot)
                for hd in range(XH if KSTOP >= 4 else 0):
                    ps, kps = pfrot.next()
                    for kc in range(KC):
                        P.op("pe", lambda e, ps=ps, kc=kc, hd=hd: e.matmul(
                            ps[:, 0:256], wk[:, kc, hd * 128:(hd + 1) * 128], hmT[:, kc, :],
                            start=(kc == 0), stop=(kc == KC - 1)),
                            reads=[("wk", 0), ("hmT", 0), ("hmT", 1)], writes=[kps])
                    copy_op(mkT[:, hd, :], ps[:, 0:256], [kps], ["mkT"])
                for mt in range(2 if KSTOP >= 5 else 0):
                    ps, kps = pfrot.next()
                    for kc in range(KC):
                        P.op("pe", lambda e, ps=ps, kc=kc, mt=mt: e.matmul(
                            ps[:, :], hmT[:, kc, mt * 128:(mt + 1) * 128], wk[:, kc, 512:1024],
                            start=(kc == 0), stop=(kc == KC - 1)),
                            reads=[("wk", 1), ("hmT", mt)], writes=[kps])
                    copy_op(mv[:, mt, :], ps[:, :], [kps], ["mv"])

                if "A" in phases:
                    for cg in range(2):
                        load_w(wk[:, :, cg * 512:(cg + 1) * 512], w_in, 0, KC, 1024 + cg * 512, 512, ("wk", cg))
                        load_w(wv[:, :, cg * 512:(cg + 1) * 512], w_in, 0, KC, 2048 + cg * 512, 512, ("wv", cg))
                    for cg in range(2):
                        load_w(wq[:, :, cg * 512:(cg + 1) * 512], w_in, 0, KC, cg * 512, 512, ("wq", cg))
                    def load_x(t):
                        s = t % 2
                        P.dma("sp", lambda e: e.dma_start(out=xs[s][:], in_=xc[t * 128:(t + 1) * 128, :]),
                              writes=[("xs", s)])

                    def load_rope(t):
                        s = t % 2
                        P.dma("sp", lambda e: e.dma_start(out=ropet[s][:], in_=rope_d[t * 128:(t + 1) * 128, :]),
                              writes=[("ropet", s)])

                    def rope_evac(ps, kps, dstb, kdst, cg, s):
                        psv = ps[:, :].rearrange("p (h d) -> p h d", h=4)
                        dv = dstb[:, cg * 512:(cg + 1) * 512].rearrange("p (h d) -> p h d", h=4)
                        cos4 = ropet[s][:, 0:64].rearrange("p (h d) -> p h d", h=4)
                        sin4 = ropet[s][:, 64:128].rearrange("p (h d) -> p h d", h=4)
                        rt_, krt = rtrot.next()
                        rt = [rt_[:, 64 * i:64 * (i + 1)].rearrange("p (h d) -> p h d", h=4) for i in range(4)]
                        P.op("dve", lambda e: e.tensor_copy(out=dv[:, :, 32:128], in_=psv[:, :, 32:128]),
                             reads=[kps], writes=[kdst])
                        rk = ("ropet", s)
                        P.op("dve", lambda e: e.tensor_tensor(out=rt[0], in0=psv[:, :, 0:16], in1=cos4, op=ALU.mult),
                             reads=[kps, rk], writes=[krt])
                        P.op("dve", lambda e: e.tensor_tensor(out=rt[1], in0=psv[:, :, 16:32], in1=sin4, op=ALU.mult),
                             reads=[kps, rk], writes=[krt])
                        P.op("dve", lambda e: e.tensor_tensor(out=rt[2], in0=psv[:, :, 16:32], in1=cos4, op=ALU.mult),
                             reads=[kps, rk], writes=[krt])
                        P.op("dve", lambda e: e.tensor_tensor(out=rt[3], in0=psv[:, :, 0:16], in1=sin4, op=ALU.mult),
                             reads=[kps, rk], writes=[krt])
                        P.op("dve", lambda e: e.tensor_tensor(out=dv[:, :, 0:16], in0=rt[0], in1=rt[1], op=ALU.subtract),
                             reads=[krt], writes=[kdst])
                        P.op("dve", lambda e: e.tensor_tensor(out=dv[:, :, 16:32], in0=rt[2], in1=rt[3], op=ALU.add),
                             reads=[krt], writes=[kdst])

                    def proj(t, wt, kw, evac):
                        s = t % 2
                        for cg in range(2):
                            ps, kps = pfrot.next()
                            for kc in range(KC):
                                P.op("pe", lambda e, ps=ps, kc=kc, cg=cg: e.matmul(
                                    ps[:, :], hT[s][:, kc, :], wt[:, kc, cg * 512:(cg + 1) * 512],
                                    start=(kc == 0), stop=(kc == KC - 1)),
                                    reads=[(kw, cg), ("hT", s)], writes=[kps])
                            evac(ps, kps, cg)

                    def do_tile(t):
                        s = t % 2
                        own = t >= 48
                        transpose_bf(hb[s], ("hb", s), KC, hT[s][:, :, :], ("hT", s), pbrot)
                        if t % 2 == 1:
                            emit_cast([P.all[-1]])
                        if t + 1 < 64:
                            s1 = (t + 1) % 2
                            norm_tm(xs[s1][:], ("xs", s1), gbc[:, 0, :], "gbc", hb[s1][:], ("hb", s1))
                        if t + 2 < 64:
                            load_x(t + 2)
                        if t == 47:
                            P.op("dve", lambda e, s=s: e.tensor_copy(out=hTh[:, :, :], in_=hT[s][:, :, 126:128]),
                                 reads=[("hT", s)], writes=["hTh"])
                        g2 = (t // 2) % 2
                        proj(t, wk, "wk", lambda ps, kps, cg, s=s: rope_evac(ps, kps, kb[s], ("kb", s), cg, s))
                        if own:
                            proj(t, wq, "wq", lambda ps, kps, cg, s=s: rope_evac(ps, kps, qb[s], ("qb", s), cg, s))
                        proj(t, wv, "wv", lambda ps, kps, cg, s=s: copy_op(
                            vb[s][:, cg * 512:(cg + 1) * 512], ps[:, :], [kps], [("vb", s)]))
                        P.dma("sp", lambda e, s=s, t=t: e.dma_start(out=v_scr[t * 128:(t + 1) * 128, :], in_=vb[s][:]),
                              reads=[("vb", s)])
                        transpose_bf(kb[s], ("kb", s), H, kTs[g2][:, :, (t % 2) * 128:(t % 2 + 1) * 128],
                                     ("kTs", g2), pbrot)
                        if t % 2 == 1:
                            grp = t // 2
                            P.dma("sp", lambda e, g2=g2, grp=grp: e.dma_start(
                                out=kT_scr[:, grp:grp + 1, :, :].rearrange("h g d t -> d (h g) t"), in_=kTs[g2][:]),
                                reads=[("kTs", g2)])
                        if own:
                            transpose_bf(qb[s], ("qb", s), H, qTs[g2][:, :, (t % 2) * 128:(t % 2 + 1) * 128],
                                         ("qTs", g2), pbrot)
                            if t % 2 == 1:
                                grp = (t - 48) // 2
                                P.dma("sp", lambda e, g2=g2, grp=grp: e.dma_start(
                                    out=qT_scr[:, grp:grp + 1, :, :].rearrange("h g d t -> d (h g) t"), in_=qTs[g2][:]),
                                    reads=[("qTs", g2)])

                    load_x(0)
                    load_rope(0)
                    load_x(1)
                    load_rope(1)
                    norm_tm(xs[0][:], ("xs", 0), gbc[:, 0, :], "gbc", hb[0][:], ("hb", 0))
                    for t in range(64):
                        do_tile(t)
                        if t + 2 < 64:
                            load_rope(t + 2)
                P.barrier()

        if "T" in phases:
            with contextlib.ExitStack() as sa:
                kT = [sb("kT%d" % i, [128, NBLK, 256], BF16, sa) for i in range(2)]
                Vt = [sb("Vt%d" % i, [128, 64, 130], BF16, sa) for i in range(2)]
                qT = [sb("qT%d" % i, [128, NOWN], BF16, sa) for i in range(2)]
                gbias = sb("gbias", [128, 16, NBLK], F32, sa)
                kms = sb("kms", [128, NBLK], F32, sa)
                kmb = sb("kmb", [128, NBLK], BF16, sa)
                gm = sb("gm", [128, 16, NBLK], F32, sa)
                m8 = sb("m8", [128, 16, 8], F32, sa)
                thr = sb("thr", [128, 16], F32, sa)
                sel = [sb("sel%d" % i, [128, 16, NBLK], F32, sa) for i in range(2)]
                pt = [sb("pt%d" % i, [128, 512], BF16, sa) for i in range(6)]
                pd = [sb("pd%d" % i, [128, 128], BF16, sa) for i in range(4)]
                oacc = [sb("oacc%d" % i, [128, 4, 129], F32, sa) for i in range(2)]
                rec = sb("rec", [128, 4], F32, sa)
                ot = [sb("ot%d" % i, [128, 512], F32, sa) for i in range(2)]
                ats = [sb("ats%d" % i, [128, 512], BF16, sa) for i in range(2)]
                ps_s = [sa.enter_context(nc.psum_tensor("psS%d" % i, [128, 512], F32)) for i in range(2)]
                ps_o = [sa.enter_context(nc.psum_tensor("psO%d" % i, [128, 512], F32)) for i in range(6)]
                orot = Rot([(p, "psO%d" % i) for i, p in enumerate(ps_o)])
                srot = Rot([(p, "psS%d" % i) for i, p in enumerate(ps_s)])
                ptrot = Rot([(p, "pt%d" % i) for i, p in enumerate(pt)])
                pdrot = Rot([(p, "pd%d" % i) for i, p in enumerate(pd)])

                P.dma("sp", lambda e: e.dma_start(out=gbias[:], in_=gbias_d), writes=["gbias"])
                for i in range(2):
                    P.op("dve", lambda e, i=i: e.memset(Vt[i][:, :, 128:130], 1.0), writes=[("Vone", i)])

                def load_head(h):
                    b = h % 2
                    P.dma("sp", lambda e: e.dma_start(out=kT[b][:], in_=kT_scr[h:h + 1].rearrange("h g d t -> d (h g) t")),
                          writes=[("kT", b)])
                    P.dma("sp", lambda e: e.dma_start(
                        out=Vt[b][:, :, 0:128],
                        in_=v_scr[:, h * 128:(h + 1) * 128].rearrange("(t p) d -> p t d", p=128)),
                        writes=[("Vt", b)])
                    P.dma("sp", lambda e: e.dma_start(
                        out=qT[b][:].rearrange("p (g t) -> p g t", g=8),
                        in_=qT_scr[h:h + 1].rearrange("h g d t -> d (h g) t")), writes=[("qT", b)])

                obank_i = [0]

                def att_group(h, g, b, kTb, Vb, qTb, selb):
                    kkT, kV, kq, ksel = ("kT", b), ("Vt", b), ("qT", b), ("sel", b)
                    par = (h * 4 + g) % 2
                    oa = oacc[par]
                    koa4 = [("oacc", par, qi) for qi in range(4)]
                    emit_cast([P.op("dve", lambda e: e.memset(oa[:], 0.0), writes=koa4)])
                    qg = qTb[:, g * 512:(g + 1) * 512]
                    nA = NPAST + 2 * g
                    nB = nA + 1

                    def scores(n, ncols, q0):
                        res = []
                        for kt in range(2):
                            ps, kps = srot.next()
                            P.op("pe", lambda e, ps=ps, kt=kt: e.matmul(
                                ps[:, 0:ncols], kTb[:, n, kt * 128:(kt + 1) * 128], qg[:, q0:q0 + ncols],
                                start=True, stop=True), reads=[kkT, kq], writes=[kps])
                            p_, kp_ = ptrot.next()
                            P.op("act", lambda e, ps=ps, p_=p_: e.activation(
                                out=p_[:, 0:ncols], in_=ps[:, 0:ncols], func=AF.Exp, scale=SCALE),
                                reads=[kps], writes=[kp_])
                            res.append((p_, kp_))
                        return res

                    def pv_sel(n, pts, qis, col0):
                        for qi in qis:
                            bank, kbank = orot.next()
                            for kt in range(2):
                                p_, kp_ = pts[kt]
                                c = (qi - col0) * 128
                                P.op("pe", lambda e, bank=bank, p_=p_, kt=kt, c=c: e.matmul(
                                    bank[:, 0:129], p_[:, c:c + 128], Vb[:, 2 * n + kt, 0:129],
                                    start=(kt == 0), stop=(kt == 1)),
                                    reads=[kp_, kV, ("Vone", b)], writes=[kbank])
                            P.op("dve", lambda e, bank=bank, qi=qi: e.scalar_tensor_tensor(
                                out=oa[:, qi, :], in0=bank[:, 0:129], scalar=selb[:, 4 * g + qi, n:n + 1],
                                in1=oa[:, qi, :], op0=ALU.mult, op1=ALU.add),
                                reads=[kbank, ksel, koa4[qi]], writes=[koa4[qi]])

                    def pv_own(n, pts, qi, col0, two):
                        bank, kbank = orot.next()
                        c = (qi - col0) * 128
                        lst = []
                        if two:
                            lst.append((pts[0][0][:, c:c + 128], pts[0][1], 0))
                            dsrc, kd, dkt = pts[1][0], pts[1][1], 1
                        else:
                            dsrc, kd, dkt = pts[0][0], pts[0][1], 0
                        pdt, kpd = pdrot.next()
                        P.op("dve", lambda e: e.tensor_tensor(
                            out=pdt[:], in0=dsrc[:, c:c + 128], in1=trib[:], op=ALU.mult),
                            reads=[kd, "trib"], writes=[kpd])
                        lst.append((pdt[:], kpd, dkt))
                        nl = len(lst)
                        for i, (lh, klh, kt) in enumerate(lst):
                            P.op("pe", lambda e, lh=lh, kt=kt, i=i: e.matmul(
                                bank[:, 0:129], lh, Vb[:, 2 * n + kt, 0:129],
                                start=(i == 0), stop=(i == nl - 1)),
                                reads=[klh, kV, ("Vone", b)], writes=[kbank])
                        P.op("dve", lambda e: e.tensor_tensor(
                            out=oa[:, qi, :], in0=bank[:, 0:129], in1=oa[:, qi, :], op=ALU.add),
                            reads=[kbank, koa4[qi]], writes=[koa4[qi]])

                    prev = None
                    for n in range(nA):
                        cur = scores(n, 512, 0)
                        if prev is not None:
                            pv_sel(n - 1, prev, range(4), 0)
                        prev = cur
                    curA = scores(nA, 512, 0)
                    pv_sel(nA - 1, prev, range(4), 0)
                    curB = scores(nB, 256, 256)
                    pv_sel(nA, curA, (2, 3), 0)
                    pv_own(nA, curA, 0, 0, False)
                    pv_own(nA, curA, 1, 0, True)
                    pv_own(nB, curB, 2, 2, False)
                    pv_own(nB, curB, 3, 2, True)
                    P.op("dve", lambda e: e.reciprocal(out=rec[:, :], in_=oa[:, :, 128]),
                         reads=koa4, writes=["rec"])
                    o_ = ot[par]
                    ko = ("ot", par)
                    for qi in range(4):
                        P.op("dve", lambda e, qi=qi: e.tensor_scalar(
                            out=o_[:, qi * 128:(qi + 1) * 128], in0=oa[:, qi, 0:128], scalar1=rec[:, qi:qi + 1],
                            scalar2=None, op0=ALU.mult), reads=[koa4[qi], "rec"], writes=[ko])
                    tb, ktb = orot.next()
                    for qi in range(4):
                        P.op("pe", lambda e, qi=qi: e.transpose(
                            out=tb[:, qi * 128:(qi + 1) * 128], in_=o_[:, qi * 128:(qi + 1) * 128],
                            identity=idf[:]), reads=[ko, "idf"], writes=[ktb])
                    a_ = ats[par]
                    ka = ("ats", par)
                    copy_op(a_[:, :], tb[:, 0:512], [ktb], [ka])
                    P.dma("sp", lambda e: e.dma_start(
                        out=at_scr[h * 128:(h + 1) * 128, g * 512:(g + 1) * 512], in_=a_[:]), reads=[ka])

                def att_head(h):
                    b = h % 2
                    kTb, Vb, qTb, selb = kT[b], Vt[b], qT[b], sel[b]
                    kkT, kq, ksel = ("kT", b), ("qT", b), ("sel", b)
                    P.op("dve", lambda e: e.tensor_reduce(out=kms[:], in_=kTb[:], axis=AX.X, op=ALU.add),
                         reads=[kkT], writes=["kms"])
                    P.op("dve", lambda e: e.tensor_scalar(out=kmb[:], in0=kms[:], scalar1=1.0 / 256.0, scalar2=None,
                                                          op0=ALU.mult), reads=["kms"], writes=["kmb"])
                    ps_g, kpg = orot.next()
                    for qt in range(16):
                        P.op("pe", lambda e, qt=qt: e.matmul(
                            ps_g[:, qt * 32:(qt + 1) * 32], qTb[:, qt * 128:(qt + 1) * 128], kmb[:, :],
                            start=True, stop=True), reads=[kq, "kmb"], writes=[kpg])
                    P.op("dve", lambda e: e.tensor_tensor(out=gm[:].rearrange("p a b -> p (a b)"), in0=ps_g[:, :],
                                                          in1=gbias[:].rearrange("p a b -> p (a b)"), op=ALU.add),
                         reads=[kpg, "gbias"], writes=["gm"])
                    for qt in range(16):
                        P.op("dve", lambda e, qt=qt: e.max(out=m8[:, qt, :], in_=gm[:, qt, :]),
                             reads=["gm"], writes=["m8"])
                    P.op("dve", lambda e: e.tensor_scalar(out=thr[:], in0=m8[:, :, 2], scalar1=-1.0e29, scalar2=None,
                                                          op0=ALU.max), reads=["m8"], writes=["thr"])
                    for qt in range(16):
                        P.op("dve", lambda e, qt=qt: e.tensor_scalar(
                            out=selb[:, qt, :], in0=gm[:, qt, :], scalar1=thr[:, qt:qt + 1], scalar2=None,
                            op0=ALU.is_ge), reads=["gm", "thr"], writes=[ksel])
                    for g in range(4):
                        att_group(h, g, b, kTb, Vb, qTb, selb)

                load_head(0)
                for h in range(H):
                    if h + 1 < H:
                        load_head(h + 1)
                    att_head(h)
                while cast_i[0] < len(CASTS):
                    emit_cast([])
                P.barrier()

        if "B" in phases:
            with contextlib.ExitStack() as sa:
                xT = sb("xT", [128, KC, 512], F32, sa)
                hTb = sb("hTB", [128, KC, 512], BF16, sa)
                R = sb("R", [128, 24704], BF16, sa)
                pan = [sb("pan%d" % i, [128, KC * 512], BF16, sa) for i in range(4)]
                xsb = sb("xsB", [128, D], F32, sa)
                sqb = [sb("sqb%d" % i, [128, 2, 512], BF16, sa) for i in range(2)]
                rstd = sb("rstd", [128, 512], F32, sa)
                tmpf = [sb("tmpf%d" % i, [128, 512], F32, sa) for i in range(4)]
                px = [sb("px%d" % i, [128, 512], BF16, sa) for i in range(4)]
                hq = sb("hq", [128, XH, 512], BF16, sa)
                xo = sb("xo", [128, XH, 512], BF16, sa)
                cxh = sb("cxh", [128, 8, 2], F32, sa)
                uh = sb("uh", [128, 8, 2], BF16, sa)
                psf = [sa.enter_context(nc.psum_tensor("psfB%d" % i, [128, 512], F32)) for i in range(8)]
                pfrot = Rot([(p, "pfB%d" % i) for i, p in enumerate(psf)])
                panrot = Rot([(p, "pan%d" % i) for i, p in enumerate(pan)])
                tfrot = Rot([(p, "tmpf%d" % i) for i, p in enumerate(tmpf)])
                pxrot = Rot([(p, "px%d" % i) for i, p in enumerate(px)])

                cxs = R[:, 0:4096].rearrange("p (m t) -> p m t", m=8)
                ycT = R[:, 4096:8192].rearrange("p (m t) -> p m t", m=8)
                atT = R[:, 8192:12288].rearrange("p (m t) -> p m t", m=8)
                uc = R[:, 12288:16400].rearrange("p (m t) -> p m t", m=8)
                mg = R[:, 16512:24704].rearrange("p (m t) -> p m t", m=16)
                uT = R[:, 0:16384].rearrange("p (f t) -> p f t", f=32)
                RK = ["Rcxs", "Ryc", "Rat", "Ru"]

                def norm_fm(gi, out3, kout):
                    ps, kps = pfrot.next()
                    for q in range(8):
                        s = sqb[q % 2]
                        ks = ("sqb", q % 2)
                        P.op("act", lambda e, s=s, q=q: e.activation(out=s[:, :, :], in_=xT[:, 2 * q:2 * q + 2, :],
                                                                     func=AF.Square), reads=[("xT", 2 * q), ("xT", 2 * q + 1)], writes=[ks])
                        for j in range(2):
                            P.op("pe", lambda e, s=s, j=j, q=q, ps=ps: e.matmul(
                                ps[:, :], onesb[:, :], s[:, j, :], start=(q == 0 and j == 0),
                                stop=(q == 7 and j == 1)), reads=[ks, "onesb"], writes=[kps])
                    P.op("dve", lambda e, ps=ps: e.tensor_scalar(out=rstd[:], in0=ps[:, :], scalar1=1.0 / D, scalar2=EPS,
                                                                 op0=ALU.mult, op1=ALU.add), reads=[kps], writes=["rstd"])
                    P.op("act", lambda e: e.activation(out=rstd[:], in_=rstd[:], func=AF.Sqrt),
                         reads=["rstd"], writes=["rstd"])
                    P.op("dve", lambda e: e.reciprocal(out=rstd[:], in_=rstd[:]), reads=["rstd"], writes=["rstd"])
                    for kc in range(KC):
                        P.op("dve", lambda e, kc=kc: e.scalar_tensor_tensor(
                            out=out3[:, kc, :], in0=xT[:, kc, :], scalar=gT[:, gi, kc:kc + 1], in1=rstd[:],
                            op0=ALU.mult, op1=ALU.mult), reads=[("xT", kc), "gT", "rstd"], writes=[(kout, kc)])

                slot_i = [0]

                def get_panel(wap, r0, kcn, c0, ncol=512):
                    slot = slot_i[0] % len(PANELS)
                    slot_i[0] += 1
                    spec = PANELS[slot]
                    assert len(spec) == 1 and spec[0][1:5] == (r0, kcn, c0, ncol) and spec[0][0] is wap, (slot, spec)
                    p_, kp_ = panrot.next()
                    n = kcn * ncol
                    P.dma("pool", lambda e: e.dma_start(out=p_[:, 0:n], in_=wscr[slot, :, 0:n]), writes=[kp_])
                    return p_[:, 0:n].rearrange("p (k n) -> p k n", k=kcn), kp_

                def lin_fm(pv, kpv, kcn, m, rhs_fn, rkeys, ps, kps, ncols=512, first=True, last=True):
                    for kc in range(kcn):
                        P.op("pe", lambda e, kc=kc: e.matmul(
                            ps[:, 0:ncols], pv[:, kc, m * 128:(m + 1) * 128], rhs_fn(kc),
                            start=(first and kc == 0), stop=(last and kc == kcn - 1)),
                            reads=[kpv] + (rkeys(kc) if callable(rkeys) else rkeys), writes=[kps])

                for g in range(4):
                    for ti in range(4):
                        t = 48 + 4 * g + ti
                        P.dma("sp", lambda e, t=t: e.dma_start(out=xsb[:], in_=xc[t * 128:(t + 1) * 128, :]),
                              writes=["xsB"])
                        for q in range(4):
                            ps, kps = pfrot.next()
                            for k in range(4):
                                P.op("pe", lambda e, ps=ps, k=k, q=q: e.transpose(
                                    out=ps[:, k * 128:(k + 1) * 128], in_=xsb[:, (4 * q + k) * 128:(4 * q + k + 1) * 128],
                                    identity=idf[:]), reads=["xsB", "idf"], writes=[kps])
                            copy_op(xT[:, 4 * q:4 * q + 4, ti * 128:(ti + 1) * 128],
                                    ps[:, :].rearrange("p (k t) -> p k t", k=4), [kps], [("xT", 4 * q + k) for k in range(4)])
                    norm_fm(0, hTb, "hT")
                    if g > 0:
                        P.op("dve", lambda e: e.tensor_copy(out=uc[:, :, 0:2], in_=uh[:, :, :]),
                             reads=["uh"], writes=["Ru"])
                    for pg in range(2):
                        pv, kpv = get_panel(w_in, 0, KC, 3072 + pg * 512)
                        for mm in range(4):
                            m = pg * 4 + mm
                            ps, kps = pfrot.next()
                            lin_fm(pv, kpv, KC, mm, lambda kc: hTb[:, kc, :], (lambda kc: [("hT", kc)]), ps, kps)
                            copy_op(cxs[:, m, :], ps[:, :], [kps], ["Rcxs"])
                            if g == 0:
                                ps2, kps2 = pfrot.next()
                                lin_fm(pv, kpv, KC, mm, lambda kc: hTh[:, kc, :], ["hTh"], ps2, kps2, ncols=2)
                                copy_op(cxh[:, m, :], ps2[:, 0:2], [kps2], ["cxh"])
                    for pg in range(2):
                        pv, kpv = get_panel(w_in, 0, KC, 5120 + pg * 512)
                        for mm in range(4):
                            m = pg * 4 + mm
                            ps, kps = pfrot.next()
                            lin_fm(pv, kpv, KC, mm, lambda kc: hTb[:, kc, :], (lambda kc: [("hT", kc)]), ps, kps)
                            P.op("dve", lambda e, ps=ps, m=m: e.tensor_tensor(
                                out=uc[:, m, 2:514], in0=ps[:, :], in1=cxs[:, m, :], op=ALU.mult),
                                reads=[kps, "Rcxs"], writes=["Ru"])
                            if g == 0:
                                ps2, kps2 = pfrot.next()
                                lin_fm(pv, kpv, KC, mm, lambda kc: hTh[:, kc, :], ["hTh"], ps2, kps2, ncols=2)
                                P.op("dve", lambda e, ps2=ps2, m=m: e.tensor_tensor(
                                    out=uc[:, m, 0:2], in0=ps2[:, 0:2], in1=cxh[:, m, :], op=ALU.mult),
                                    reads=[kps2, "cxh"], writes=["Ru"])
                    P.op("dve", lambda e: e.tensor_copy(out=uh[:, :, :], in_=uc[:, :, 512:514]),
                         reads=["Ru"], writes=["uh"])
                    for pg in range(2):
                        pv, kpv = get_panel(w_in, 0, KC, 4096 + pg * 512)
                        for mm in range(4):
                            m = pg * 4 + mm
                            ps, kps = pfrot.next()
                            lin_fm(pv, kpv, KC, mm, lambda kc: hTb[:, kc, :], (lambda kc: [("hT", kc)]), ps, kps)
                            y, ky = tfrot.next()
                            P.op("dve", lambda e, y=y, m=m: e.tensor_scalar(
                                out=y[:], in0=uc[:, m, 2:514], scalar1=cwT[:, m, 2:3], scalar2=None, op0=ALU.mult),
                                reads=["Ru", "cwT"], writes=[ky])
                            P.op("dve", lambda e, y=y, m=m: e.scalar_tensor_tensor(
                                out=y[:], in0=uc[:, m, 1:513], scalar=cwT[:, m, 1:2], in1=y[:],
                                op0=ALU.mult, op1=ALU.add), reads=["Ru", "cwT", ky], writes=[ky])
                            P.op("dve", lambda e, y=y, m=m: e.scalar_tensor_tensor(
                                out=y[:], in0=uc[:, m, 0:512], scalar=cwT[:, m, 0:1], in1=y[:],
                                op0=ALU.mult, op1=ALU.add), reads=["Ru", "cwT", ky], writes=[ky])
                            P.op("dve", lambda e, y=y, m=m, ps=ps: e.tensor_tensor(
                                out=ycT[:, m, :], in0=ps[:, :], in1=y[:], op=ALU.mult),
                                reads=[kps, ky], writes=["Ryc"])
                    P.dma("sp", lambda e, g=g: e.dma_start(
                        out=atT, in_=at_scr[:, g * 512:(g + 1) * 512].rearrange("(m p) t -> p m t", p=128)),
                        writes=["Rat"])
                    for cgp in range(4):
                        pga, kga = get_panel(w_in, 0, KC, 6144 + cgp * 512)
                        pgc, kgc = get_panel(w_in, 0, KC, 8192 + cgp * 512)
                        slot = slot_i[0] % len(PANELS)
                        slot_i[0] += 1
                        assert len(PANELS[slot]) == 2
                        p_, kp_ = panrot.next()
                        pao = p_[:, 0:4096].rearrange("p (k n) -> p k n", k=8)
                        pco = p_[:, 4096:8192].rearrange("p (k n) -> p k n", k=8)
                        P.dma("pool", lambda e, p_=p_, slot=slot: e.dma_start(out=p_[:, :], in_=wscr[slot, :, :]),
                              writes=[kp_])
                        for mm in range(4):
                            m = cgp * 4 + mm
                            psa, ka = pfrot.next()
                            lin_fm(pga, kga, KC, mm, lambda kc: hTb[:, kc, :], (lambda kc: [("hT", kc)]), psa, ka)
                            psc, kc_ = pfrot.next()
                            lin_fm(pgc, kgc, KC, mm, lambda kc: hTb[:, kc, :], (lambda kc: [("hT", kc)]), psc, kc_)
                            psy, ky_ = pfrot.next()
                            lin_fm(pao, kp_, 8, mm, lambda kc: atT[:, kc, :], ["Rat"], psy, ky_)
                            psz, kz_ = pfrot.next()
                            lin_fm(pco, kp_, 8, mm, lambda kc: ycT[:, kc, :], ["Ryc"], psz, kz_)
                            sa_, ksa = tfrot.next()
                            sc_, ksc = tfrot.next()
                            P.op("act", lambda e, sa_=sa_, psa=psa: e.activation(out=sa_[:], in_=psa[:, :], func=AF.Sigmoid),
                                 reads=[ka], writes=[ksa])
                            P.op("act", lambda e, sc_=sc_, psc=psc: e.activation(out=sc_[:], in_=psc[:, :], func=AF.Sigmoid),
                                 reads=[kc_], writes=[ksc])
                            P.op("dve", lambda e, sa_=sa_, psy=psy: e.tensor_tensor(out=sa_[:], in0=psy[:, :], in1=sa_[:], op=ALU.mult),
                                 reads=[ky_, ksa], writes=[ksa])
                            P.op("dve", lambda e, sc_=sc_, psz=psz: e.tensor_tensor(out=sc_[:], in0=psz[:, :], in1=sc_[:], op=ALU.mult),
                                 reads=[kz_, ksc], writes=[ksc])
                            P.op("dve", lambda e, sa_=sa_, sc_=sc_, m=m: e.tensor_tensor(out=mg[:, m, :], in0=sa_[:], in1=sc_[:], op=ALU.add),
                                 reads=[ksa, ksc], writes=["Rmg"])

                    def resid_layer(wap, kcn, rhs3, rkeys):
                        for cgp in range(4):
                            pv, kpv = get_panel(wap, 0, kcn, cgp * 512)
                            for mm in range(4):
                                m = cgp * 4 + mm
                                ps, kps = pfrot.next()
                                lin_fm(pv, kpv, kcn, mm, lambda kc: rhs3[:, kc, :], rkeys, ps, kps)
                                P.op("dve", lambda e, ps=ps, m=m: e.tensor_tensor(
                                    out=xT[:, m, :], in0=ps[:, :], in1=xT[:, m, :], op=ALU.add),
                                    reads=[kps, ("xT", m)], writes=[("xT", m)])

                    resid_layer(w_mix, KC, mg, ["Rmg"])
                    norm_fm(1, hTb, "hT")
                    pv, kpv = get_panel(wq_x, 0, KC, 0)
                    for hd in range(XH):
                        ps, kps = pfrot.next()
                        lin_fm(pv, kpv, KC, hd, lambda kc: hTb[:, kc, :], (lambda kc: [("hT", kc)]), ps, kps)
                        copy_op(hq[:, hd, :], ps[:, :], [kps], ["hq"])
                    for hd in range(XH):
                        pxs = []
                        for mt in range(2):
                            ps, kps = pfrot.next()
                            P.op("pe", lambda e, ps=ps, mt=mt, hd=hd: e.matmul(
                                ps[:, :], mkT[:, hd, mt * 128:(mt + 1) * 128], hq[:, hd, :], start=True, stop=True),
                                reads=["mkT", "hq"], writes=[kps])
                            p_, kp_ = pxrot.next()
                            P.op("act", lambda e, ps=ps, p_=p_: e.activation(out=p_[:], in_=ps[:, :], func=AF.Exp, scale=SCALE),
                                 reads=[kps], writes=[kp_])
                            pxs.append((p_, kp_))
                        pso, kpo = pfrot.next()
                        psz, kpz = pfrot.next()
                        for mt in range(2):
                            p_, kp_ = pxs[mt]
                            P.op("pe", lambda e, pso=pso, mt=mt, hd=hd, p_=p_: e.matmul(
                                pso[:, :], mv[:, mt, hd * 128:(hd + 1) * 128], p_[:], start=(mt == 0), stop=(mt == 1)),
                                reads=["mv", kp_], writes=[kpo])
                        for mt in range(2):
                            p_, kp_ = pxs[mt]
                            P.op("pe", lambda e, psz=psz, mt=mt, p_=p_: e.matmul(
                                psz[:, :], onesb[:, :], p_[:], start=(mt == 0), stop=(mt == 1)),
                                reads=["onesb", kp_], writes=[kpz])
                        rz, krz = tfrot.next()
                        P.op("dve", lambda e, rz=rz, psz=psz: e.reciprocal(out=rz[:], in_=psz[:, :]), reads=[kpz], writes=[krz])
                        P.op("dve", lambda e, rz=rz, pso=pso, hd=hd: e.tensor_tensor(
                            out=xo[:, hd, :], in0=pso[:, :], in1=rz[:], op=ALU.mult), reads=[kpo, krz], writes=["xo"])
                    resid_layer(wo_x, XH, xo, ["xo"])
                    norm_fm(2, hTb, "hT")
                    for half in range(2):
                        for pg in range(8):
                            pv, kpv = get_panel(w_up, 0, KC, half * 4096 + pg * 512)
                            for mm in range(4):
                                f = pg * 4 + mm
                                ps, kps = pfrot.next()
                                lin_fm(pv, kpv, KC, mm, lambda kc: hTb[:, kc, :], (lambda kc: [("hT", kc)]), ps, kps)
                                sq, ksq = tfrot.next()
                                P.op("act", lambda e, sq=sq, ps=ps: e.activation(out=sq[:], in_=ps[:, :], func=AF.Square),
                                     reads=[kps], writes=[ksq])
                                P.op("dve", lambda e, sq=sq, ps=ps, f=f: e.scalar_tensor_tensor(
                                    out=uT[:, f, :], in0=ps[:, :], scalar=0.0, in1=sq[:], op0=ALU.is_gt, op1=ALU.mult),
                                    reads=[kps, ksq], writes=[RK[f // 8]])
                        for cgp in range(4):
                            banks = [pfrot.next() for _ in range(4)]
                            for qd in range(2):
                                pv, kpv = get_panel(w_down, half * 4096 + qd * 2048, KC, cgp * 512)
                                for mm in range(4):
                                    ps, kps = banks[mm]
                                    lin_fm(pv, kpv, KC, mm, lambda kc, qd=qd: uT[:, qd * 16 + kc, :],
                                           [RK[2 * qd], RK[2 * qd + 1]], ps, kps, first=(qd == 0), last=(qd == 1))
                            for mm in range(4):
                                m = cgp * 4 + mm
                                ps, kps = banks[mm]
                                P.op("dve", lambda e, ps=ps, m=m: e.tensor_tensor(
                                    out=xT[:, m, :], in0=ps[:, :], in1=xT[:, m, :], op=ALU.add),
                                    reads=[kps, ("xT", m)], writes=[("xT", m)])
                    norm_fm(3, xT, "xT")
                    for ti in range(4):
                        for q in range(4):
                            ps, kps = pfrot.next()
                            for k in range(4):
                                P.op("pe", lambda e, ps=ps, k=k, q=q, ti=ti: e.transpose(
                                    out=ps[:, k * 128:(k + 1) * 128], in_=xT[:, 4 * q + k, ti * 128:(ti + 1) * 128],
                                    identity=idf[:]), reads=[("xT", 4 * q + k), "idf"], writes=[kps])
                            copy_op(xsb[:, q * 512:(q + 1) * 512], ps[:, :], [kps], ["xsB"])
                        r0 = g * 512 + ti * 128
                        P.dma("sp", lambda e, r0=r0: e.dma_start(out=out_d[r0:r0 + 128, :], in_=xsb[:]), reads=["xsB"])

        P.finalize()
        esems = {en: es.enter_context(nc.semaphore("es_" + en)) for en in P.ENGS}
        dsems = {k: es.enter_context(nc.semaphore("ds_%s_%d" % k)) for k in P.dma_uses}
        block = es.enter_context(nc.Block())

        @block.sync
        def _(e):
            P.emit_engine("sp", e, esems, dsems)

        @block.scalar
        def _(e):
            P.emit_engine("act", e, esems, dsems)

        @block.vector
        def _(e):
            P.emit_engine("dve", e, esems, dsems)

        @block.gpsimd
        def _(e):
            P.emit_engine("pool", e, esems, dsems)

        @block.tensor
        def _(e):
            P.emit_engine("pe", e, esems, dsems)
    return nc


def _consts(j):
    half = 16
    inv_freq = (500000.0 ** (-np.arange(half, dtype=np.float32) / half)).astype(np.float32)
    pos = (np.arange(NCTX) + (j - 3) * NOWN).astype(np.float32)
    ang = pos[:, None] * inv_freq[None, :]
    cos, sin = np.cos(ang).astype(np.float32), np.sin(ang).astype(np.float32)
    rope = np.concatenate([np.tile(cos, (1, 4)), np.tile(sin, (1, 4))], axis=1).astype(np.float32)
    gb = np.full((16, NBLK), NEG, np.float32)
    first_valid = NPAST - 8 * j
    for qt in range(16):
        ob = NPAST + qt // 2
        gb[qt, first_valid:ob] = 0.0
    gbias = np.ascontiguousarray(np.broadcast_to(gb[None], (128, 16, NBLK))).astype(np.float32)
    return rope, gbias


_NC_CACHE = {}


def make_in_maps(inputs):
    f = lambda k: np.ascontiguousarray(np.asarray(inputs[k], dtype=np.float32))
    x, mem = f("x"), f("mem")
    shared = {
        "w_in": f("w_in")[0], "w_attn_out": f("w_attn_out")[0], "w_conv_out": f("w_conv_out")[0],
        "w_mix_out": f("w_mix_out")[0], "wq_x": f("wq_x")[0], "wkv_x": f("wkv_x")[0], "wo_x": f("wo_x")[0],
        "w_up": f("w_up")[0], "w_down": f("w_down")[0],
    }
    gbc = np.stack([np.broadcast_to(f("norm_mix")[0][None], (128, D)),
                    np.broadcast_to(f("norm_mem")[0][None], (128, D))]).astype(np.float32)
    gT = np.stack([f("norm_mix")[0], f("norm_xattn")[0], f("norm_mlp")[0], f("norm_final")], 0)
    gT = np.ascontiguousarray(gT.reshape(4, KC, 128).transpose(2, 0, 1))
    cwT = np.ascontiguousarray(f("conv_w")[0].reshape(3, 8, 128).transpose(2, 1, 0))
    shared.update({"gbc": np.ascontiguousarray(gbc), "gT": gT, "cwT": cwT,
                   "ident": np.eye(128, dtype=np.float32),
                   "tri": np.triu(np.ones((128, 128), np.float32))})
    in_maps = []
    for c in range(8):
        b, j = c // 4, c % 4
        xcw = np.zeros((NCTX, D), np.float32)
        n = (j + 1) * NOWN
        xcw[NCTX - n:] = x[b, :n]
        rope, gbias = _consts(j)
        m = dict(shared)
        m.update({"xc": xcw, "mem": mem[b], "rope": rope, "gbias": gbias})
        in_maps.append(m)
    return in_maps


def kernel(**inputs):
    if "nc" not in _NC_CACHE:
        _NC_CACHE["nc"] = build()
    nc = _NC_CACHE["nc"]
    in_maps = make_in_maps(inputs)
    res = run_bass_kernel_spmd(nc, in_maps, core_ids=list(range(8)))
    out = np.zeros((2, 8192, D), np.float32)
    for c in range(8):
        b, j = c // 4, c % 4
        out[b, j * NOWN:(j + 1) * NOWN] = np.asarray(res.results[c]["out"], dtype=np.float32)
    return out
```

```python
import contextlib

import numpy as np

import concourse.bass as bass
import concourse.mybir as mybir
from concourse.bass_utils import run_bass_kernel_spmd

F32 = mybir.dt.float32
BF16 = mybir.dt.bfloat16
ALU = mybir.AluOpType
AF = mybir.ActivationFunctionType
AX = mybir.AxisListType

D = 2048
KC = 16
NCTX = 8192
NOWN = 2048
H = 8
DH = 128
NBLK = 32
NPAST = 24
XH = 4
DFF = 8192
EPS = 1e-6
NEG = -1.0e30
SCALE = DH ** -0.5


class Ins:
    __slots__ = ("eng", "emit", "deps", "mark", "cnt", "is_dma", "dsem", "dval", "gid")

    def __init__(self, eng, emit, is_dma=False):
        self.eng = eng
        self.emit = emit
        self.deps = []
        self.mark = False
        self.cnt = 0
        self.is_dma = is_dma
        self.dsem = None
        self.dval = 0
        self.gid = 0


class Planner:
    ENGS = ("pe", "act", "dve", "pool", "sp")

    def __init__(self, n_dma_sems=12):
        self.streams = {e: [] for e in self.ENGS}
        self.reg = {}
        self.n_dma_sems = n_dma_sems
        self.dma_rr = {e: 0 for e in self.ENGS}
        self.dma_last = {}
        self.dma_uses = {}
        self.all = []
        self.open_dmas = []

    def _add(self, ins, reads, writes):
        deps = set()
        for k in reads:
            st = self.reg.get(k)
            if st is not None and st[0] is not None:
                deps.add(st[0])
            if st is not None and isinstance(k, str) and k.startswith("p"):
                deps.update(v for en, v in st[1].items() if en != ins.eng)
        for k in writes:
            st = self.reg.get(k)
            if st is not None:
                if st[0] is not None:
                    deps.add(st[0])
                deps.update(st[1].values())
                deps.update(st[2])
        if ins.eng == "pe" and not ins.is_dma:
            deps = {d for d in deps if not (d.eng == "pe" and not d.is_dma)}
        deps.discard(ins)
        ins.deps = list(deps)
        for k in reads:
            st = self.reg.setdefault(k, [None, {}, []])
            if ins.is_dma:
                st[2].append(ins)
            else:
                st[1][ins.eng] = ins
        for k in writes:
            self.reg[k] = [ins, {}, []]
        ins.gid = len(self.all)
        self.all.append(ins)
        self.streams[ins.eng].append(ins)
        return ins

    def op(self, eng, emit, reads=(), writes=()):
        return self._add(Ins(eng, emit), reads, writes)

    def dma(self, eng, emit, reads=(), writes=(), after=()):
        ins = Ins(eng, emit, is_dma=True)
        slot = self.dma_rr[eng] % self.n_dma_sems
        self.dma_rr[eng] += 1
        key = (eng, slot)
        n = self.dma_uses.get(key, 0) + 1
        self.dma_uses[key] = n
        ins.dsem = key
        ins.dval = 16 * n
        self._add(ins, reads, writes)
        ins.deps.extend(after)
        prev = self.dma_last.get(key)
        if prev is not None:
            ins.deps.append(prev)
        self.dma_last[key] = ins
        self.open_dmas.append(ins)
        return ins

    def barrier(self):
        deps = []
        for st in self.streams.values():
            for ins in reversed(st):
                if not ins.is_dma and ins.emit is not None:
                    deps.append(ins)
                    break
        deps = deps + list(self.open_dmas)
        self.open_dmas = []
        for e in self.ENGS:
            ins = Ins(e, None)
            ins.deps = [d for d in deps]
            ins.gid = len(self.all)
            self.all.append(ins)
            self.streams[e].append(ins)
        self.reg = {}

    def finalize(self):
        for ins in self.all:
            for d in ins.deps:
                d.mark = True
        for e in self.ENGS:
            c = 0
            for ins in self.streams[e]:
                if ins.mark and not ins.is_dma:
                    c += 1
                    ins.cnt = c

    def emit_engine(self, eng, e, esems, dsems):
        known = {}
        stream = self.streams[eng]
        look = 0

        def kv(d):
            if d.is_dma:
                return ("d",) + d.dsem, d.dval
            return ("e", d.eng), d.cnt

        for pos, ins in enumerate(stream):
            need = {}
            for d in ins.deps:
                key, val = kv(d)
                if need.get(key, 0) < val:
                    need[key] = val
            for key, val in need.items():
                if known.get(key, 0) >= val:
                    continue
                for nxt in stream[pos + 1:pos + 1 + look]:
                    for d in nxt.deps:
                        if d.gid < ins.gid and not d.is_dma:
                            k2, v2 = kv(d)
                            if k2 == key and v2 > val:
                                val = v2
                known[key] = val
                sem = dsems[key[1:]] if key[0] == "d" else esems[key[1]]
                e.wait_ge(sem, val)
            if ins.emit is None:
                continue
            bi = ins.emit(e)
            if ins.is_dma:
                bi.then_inc(dsems[ins.dsem], 16)
            elif ins.mark:
                bi.then_inc(esems[eng], 1)
        for (de, slot), n in self.dma_uses.items():
            if de == eng:
                key = ("d", de, slot)
                if known.get(key, 0) < 16 * n:
                    e.wait_ge(dsems[(de, slot)], 16 * n)


class Rot:
    def __init__(self, items):
        self.items = items
        self.i = 0

    def next(self):
        it = self.items[self.i % len(self.items)]
        self.i += 1
        return it


def build(debug=False, phases="MATB"):
    nc = bass.Bass("TRN2", target_bir_lowering=False)

    def din(name, shape, dt=F32):
        return nc.dram_tensor(name, list(shape), dt, kind="ExternalInput").ap()

    xc = din("xc", [NCTX, D])
    mem = din("mem", [256, D])
    w_in = din("w_in", [D, 10240])
    w_ao = din("w_attn_out", [1024, D])
    w_co = din("w_conv_out", [1024, D])
    w_mix = din("w_mix_out", [D, D])
    wq_x = din("wq_x", [D, 512])
    wkv_x = din("wkv_x", [D, 1024])
    wo_x = din("wo_x", [512, D])
    w_up = din("w_up", [D, DFF])
    w_down = din("w_down", [DFF, D])
    gbc_d = din("gbc", [2, 128, D])
    gT_d = din("gT", [128, 4, KC])
    cwT_d = din("cwT", [128, 8, 3])
    rope_d = din("rope", [NCTX, 128])
    gbias_d = din("gbias", [128, 16, NBLK])
    ident_d = din("ident", [128, 128])
    tri_d = din("tri", [128, 128])

    skind = "ExternalOutput" if debug else "Internal"
    kT_scr = nc.dram_tensor("kT_scr", [H, 32, 128, 256], BF16, kind=skind).ap()
    v_scr = nc.dram_tensor("v_scr", [NCTX, 1024], BF16, kind=skind).ap()
    qT_scr = nc.dram_tensor("qT_scr", [H, 8, 128, 256], BF16, kind=skind).ap()
    at_scr = nc.dram_tensor("at_scr", [1024, NOWN], BF16, kind=skind).ap()
    out_d = nc.dram_tensor("out", [NOWN, D], F32, kind="ExternalOutput").ap()
    wscr = nc.dram_tensor("wscr", [59, 128, 8192], BF16, kind="Internal").ap()

    def panel_list():
        L = []
        for c0 in (3072, 5120, 4096):
            for pg in range(2):
                L.append([(w_in, 0, KC, c0 + pg * 512, 512, 0)])
        for cgp in range(4):
            L.append([(w_in, 0, KC, 6144 + cgp * 512, 512, 0)])
            L.append([(w_in, 0, KC, 8192 + cgp * 512, 512, 0)])
            L.append([(w_ao, 0, 8, cgp * 512, 512, 0), (w_co, 0, 8, cgp * 512, 512, 4096)])
        for cgp in range(4):
            L.append([(w_mix, 0, KC, cgp * 512, 512, 0)])
        L.append([(wq_x, 0, KC, 0, 512, 0)])
        for cgp in range(4):
            L.append([(wo_x, 0, XH, cgp * 512, 512, 0)])
        for half in range(2):
            for pg in range(8):
                L.append([(w_up, 0, KC, half * 4096 + pg * 512, 512, 0)])
            for cgp in range(4):
                for qd in range(2):
                    L.append([(w_down, half * 4096 + qd * 2048, KC, cgp * 512, 512, 0)])
        return L

    PANELS = panel_list()
    CASTS = [(slot,) + part for slot, parts in enumerate(PANELS) for part in parts]
    cast_i = [0]

    P = Planner()
    es = contextlib.ExitStack()
    with es:
        def sb(name, shape, dt, stack=es):
            return stack.enter_context(nc.sbuf_tensor("s_" + name, list(shape), dt))

        idf = sb("idf", [128, 128], F32)
        idb = sb("idb", [128, 128], BF16)
        trib = sb("trib", [128, 128], BF16)
        onesb = sb("onesb", [128, 128], BF16)
        gT = sb("gT", [128, 4, KC], F32)
        cwT = sb("cwT", [128, 8, 3], F32)
        mkT = sb("mkT", [128, XH, 256], BF16)
        mv = sb("mv", [128, 2, 512], BF16)
        hTh = sb("hTh", [128, KC, 2], BF16)
        stat = sb("stat", [128, 256], F32)

        P.dma("sp", lambda e: e.dma_start(out=idf[:], in_=ident_d), writes=["idf"])
        P.dma("sp", lambda e: e.dma_start(out=gT[:], in_=gT_d), writes=["gT"])
        P.dma("sp", lambda e: e.dma_start(out=cwT[:], in_=cwT_d), writes=["cwT"])
        P.op("dve", lambda e: e.tensor_copy(out=idb[:], in_=idf[:]), reads=["idf"], writes=["idb"])
        P.op("dve", lambda e: e.memset(onesb[:], 1.0), writes=["onesb"])
        P.op("dve", lambda e: e.memset(stat[:], 0.0), writes=["statinit"])

        stat_i = [0]
        cp_i = [0]

        def emit_cast(after):
            if cast_i[0] >= len(CASTS):
                return
            slot, wap, r0, kcn, c0, ncol, off = CASTS[cast_i[0]]
            cast_i[0] += 1
            src = wap[r0:r0 + kcn * 128, c0:c0 + ncol].rearrange("(kc p) n -> p kc n", p=128)
            dst = wscr[slot, :, off:off + kcn * ncol].rearrange("p (k n) -> p k n", k=kcn)
            P.dma("pool", lambda e: e.dma_start(out=dst, in_=src), writes=[("wscr", slot, off)], after=after)

        def copy_op(out, in_, reads, writes):
            cp_i[0] += 1
            if cp_i[0] % 2:
                P.op("act", lambda e: e.activation(out=out, in_=in_, func=AF.Copy), reads=reads, writes=writes)
            else:
                P.op("dve", lambda e: e.tensor_copy(out=out, in_=in_), reads=reads, writes=writes)

        def norm_tm(xs, kx, gbc, kg, hb, khb):
            i = stat_i[0]
            stat_i[0] += 1
            ks = ("stat", i)
            c0 = 3 * i
            P.op("act", lambda e: e.activation(out=junk[:], in_=xs, func=AF.Square,
                                               accum_out=stat[:, c0:c0 + 1]),
                 reads=[kx, "statinit"], writes=[ks])
            P.op("dve", lambda e: e.tensor_scalar(out=stat[:, c0 + 1:c0 + 2], in0=stat[:, c0:c0 + 1],
                                                  scalar1=1.0 / D, scalar2=EPS, op0=ALU.mult, op1=ALU.add),
                 reads=[ks], writes=[ks])
            P.op("act", lambda e: e.activation(out=stat[:, c0 + 2:c0 + 3], in_=stat[:, c0 + 1:c0 + 2],
                                               func=AF.Sqrt), reads=[ks], writes=[ks])
            P.op("dve", lambda e: e.reciprocal(out=stat[:, c0 + 2:c0 + 3], in_=stat[:, c0 + 2:c0 + 3]),
                 reads=[ks], writes=[ks])
            P.op("dve", lambda e: e.scalar_tensor_tensor(out=hb, in0=xs, scalar=stat[:, c0 + 2:c0 + 3],
                                                         in1=gbc, op0=ALU.mult, op1=ALU.mult),
                 reads=[kx, ks, kg], writes=[khb])

        def transpose_bf(src, ksrc, n, dst3, kdst, pbrot):
            for b0 in range(0, n, 8):
                nb = min(8, n - b0)
                pb, kpb = pbrot.next()
                for k in range(nb):
                    P.op("pe", lambda e, k=k, pb=pb, b0=b0: e.transpose(out=pb[:, k * 128:(k + 1) * 128],
                                                                  in_=src[:, (b0 + k) * 128:(b0 + k + 1) * 128],
                                                                  identity=idb[:]),
                         reads=[ksrc, "idb"], writes=[kpb])
                copy_op(dst3[:, b0:b0 + nb, :], pb[:, 0:nb * 128].rearrange("p (k q) -> p k q", k=nb),
                        [kpb], [kdst])

        def load_w(dst, wap, r0, kcn, c0, ncol, key):
            src = wap[r0:r0 + kcn * 128, c0:c0 + ncol].rearrange("(kc p) n -> p kc n", p=128)
            P.dma("pool", lambda e: e.dma_start(out=dst, in_=src), writes=[key])

        if "M" in phases or "A" in phases:
            with contextlib.ExitStack() as sa:
                wq = sb("wq", [128, KC, 1024], BF16, sa)
                junk = sb("junk", [128, D], BF16, sa)
                trif = sb("trif", [128, 128], F32, sa)
                P.dma("sp", lambda e: e.dma_start(out=trif[:], in_=tri_d), writes=["trif"])
                P.op("dve", lambda e: e.tensor_copy(out=trib[:], in_=trif[:]), reads=["trif"], writes=["trib"])
                wk = sb("wk", [128, KC, 1024], BF16, sa)
                wv = sb("wv", [128, KC, 1024], BF16, sa)
                gbc = sb("gbc", [128, 2, D], F32, sa)
                xs = [sb("xs%d" % i, [128, D], F32, sa) for i in range(2)]
                hb = [sb("hb%d" % i, [128, D], BF16, sa) for i in range(2)]
                hT = [sb("hT%d" % i, [128, KC, 128], BF16, sa) for i in range(2)]
                vb = [sb("vb%d" % i, [128, 1024], BF16, sa) for i in range(2)]
                kb = [sb("kb%d" % i, [128, 1024], BF16, sa) for i in range(2)]
                qb = [sb("qb%d" % i, [128, 1024], BF16, sa) for i in range(2)]
                kTs = [sb("kTs%d" % i, [128, H, 256], BF16, sa) for i in range(2)]
                qTs = [sb("qTs%d" % i, [128, H, 256], BF16, sa) for i in range(2)]
                ropet = [sb("ropet%d" % i, [128, 128], F32, sa) for i in range(2)]
                rtmp = [sb("rtmp%d" % i, [128, 256], F32, sa) for i in range(4)]
                psf = [sa.enter_context(nc.psum_tensor("psfA%d" % i, [128, 512], F32)) for i in range(6)]
                psb = [sa.enter_context(nc.psum_tensor("psbA%d" % i, [128, 1024], BF16)) for i in range(2)]
                pfrot = Rot([(p, "pfA%d" % i) for i, p in enumerate(psf)])
                pbrot = Rot([(p, "pbA%d" % i) for i, p in enumerate(psb)])
                rtrot = Rot([(p, "rtmp%d" % i) for i, p in enumerate(rtmp)])

                P.dma("sp", lambda e: e.dma_start(out=gbc[:], in_=gbc_d.rearrange("g p d -> p g d")),
                      writes=["gbc"])
                for cg in range(2):
                    load_w(wk[:, :, cg * 512:(cg + 1) * 512], wkv_x, 0, KC, cg * 512, 512, ("wk", cg))

                hmT = sb("hmT", [128, KC, 256], BF16, sa)
                import os
                KSTOP = int(os.environ.get("KSTOP", "99"))
                for mt in range(2):
                    s = mt % 2
                    P.dma("sp", lambda e, s=s, mt=mt: e.dma_start(out=xs[s][:], in_=mem[mt * 128:(mt + 1) * 128, :]),
                          writes=[("xs", s)])
                    if KSTOP >= 2:
                        norm_tm(xs[s][:], ("xs", s), gbc[:, 1, :], "gbc", hb[s][:], ("hb", s))
                    if KSTOP >= 3:
                        transpose_bf(hb[s], ("hb", s), KC, hmT[:, :, mt * 128:(mt + 1) * 128], ("hmT", mt), pbrot)
                for hd in range(XH if KSTOP >= 4 else 0):
                    ps, kps = pfrot.next()
                    for kc in range(KC):
                        P.op("pe", lambda e, ps=ps, kc=kc, hd=hd: e.matmul(
                            ps[:, 0:256], wk[:, kc, hd * 128:(hd + 1) * 128], hmT[:, kc, :],
                            start=(kc == 0), stop=(kc == KC - 1)),
                            reads=[("wk", 0), ("hmT", 0), ("hmT", 1)], writes=[kps])
                    copy_op(mkT[:, hd, :], ps[:, 0:256], [kps], ["mkT"])
                for mt in range(2 if KSTOP >= 5 else 0):
                    ps, kps = pfrot.next()
                    for kc in range(KC):
                        P.op("pe", lambda e, ps=ps, kc=kc, mt=mt: e.matmul(
                            ps[:, :], hmT[:, kc, mt * 128:(mt + 1) * 128], wk[:, kc, 512:1024],
                            start=(kc == 0), stop=(kc == KC - 1)),
                            reads=[("wk", 1), ("hmT", mt)], writes=[kps])
                    copy_op(mv[:, mt, :], ps[:, :], [kps], ["mv"])

                if "A" in phases:
                    for cg in range(2):
                        load_w(wk[:, :, cg * 512:(cg + 1) * 512], w_in, 0, KC, 1024 + cg * 512, 512, ("wk", cg))
                        load_w(wv[:, :, cg * 512:(cg + 1) * 512], w_in, 0, KC, 2048 + cg * 512, 512, ("wv", cg))
                    for cg in range(2):
                        load_w(wq[:, :, cg * 512:(cg + 1) * 512], w_in, 0, KC, cg * 512, 512, ("wq", cg))
                    def load_x(t):
                        s = t % 2
                        P.dma("sp", lambda e: e.dma_start(out=xs[s][:], in_=xc[t * 128:(t + 1) * 128, :]),
                              writes=[("xs", s)])

                    def load_rope(t):
                        s = t % 2
                        P.dma("sp", lambda e: e.dma_start(out=ropet[s][:], in_=rope_d[t * 128:(t + 1) * 128, :]),
                              writes=[("ropet", s)])

                    def rope_evac(ps, kps, dstb, kdst, cg, s):
                        psv = ps[:, :].rearrange("p (h d) -> p h d", h=4)
                        dv = dstb[:, cg * 512:(cg + 1) * 512].rearrange("p (h d) -> p h d", h=4)
                        cos4 = ropet[s][:, 0:64].rearrange("p (h d) -> p h d", h=4)
                        sin4 = ropet[s][:, 64:128].rearrange("p (h d) -> p h d", h=4)
                        rt_, krt = rtrot.next()
                        rt = [rt_[:, 64 * i:64 * (i + 1)].rearrange("p (h d) -> p h d", h=4) for i in range(4)]
                        P.op("dve", lambda e: e.tensor_copy(out=dv[:, :, 32:128], in_=psv[:, :, 32:128]),
                             reads=[kps], writes=[kdst])
                        rk = ("ropet", s)
                        P.op("dve", lambda e: e.tensor_tensor(out=rt[0], in0=psv[:, :, 0:16], in1=cos4, op=ALU.mult),
                             reads=[kps, rk], writes=[krt])
                        P.op("dve", lambda e: e.tensor_tensor(out=rt[1], in0=psv[:, :, 16:32], in1=sin4, op=ALU.mult),
                             reads=[kps, rk], writes=[krt])
                        P.op("dve", lambda e: e.tensor_tensor(out=rt[2], in0=psv[:, :, 16:32], in1=cos4, op=ALU.mult),
                             reads=[kps, rk], writes=[krt])
                        P.op("dve", lambda e: e.tensor_tensor(out=rt[3], in0=psv[:, :, 0:16], in1=sin4, op=ALU.mult),
                             reads=[kps, rk], writes=[krt])
                        P.op("dve", lambda e: e.tensor_tensor(out=dv[:, :, 0:16], in0=rt[0], in1=rt[1], op=ALU.subtract),
                             reads=[krt], writes=[kdst])
                        P.op("dve", lambda e: e.tensor_tensor(out=dv[:, :, 16:32], in0=rt[2], in1=rt[3], op=ALU.add),
                             reads=[krt], writes=[kdst])

                    def proj(t, wt, kw, evac):
                        s = t % 2
                        for cg in range(2):
                            ps, kps = pfrot.next()
                            for kc in range(KC):
                                P.op("pe", lambda e, ps=ps, kc=kc, cg=cg: e.matmul(
                                    ps[:, :], hT[s][:, kc, :], wt[:, kc, cg * 512:(cg + 1) * 512],
                                    start=(kc == 0), stop=(kc == KC - 1)),
                                    reads=[(kw, cg), ("hT", s)], writes=[kps])
                            evac(ps, kps, cg)

                    def do_tile(t):
                        s = t % 2
                        own = t >= 48
                        transpose_bf(hb[s], ("hb", s), KC, hT[s][:, :, :], ("hT", s), pbrot)
                        if t % 2 == 1:
                            emit_cast([P.all[-1]])
                        if t + 1 < 64:
                            s1 = (t + 1) % 2
                            norm_tm(xs[s1][:], ("xs", s1), gbc[:, 0, :], "gbc", hb[s1][:], ("hb", s1))
                        if t + 2 < 64:
                            load_x(t + 2)
                        if t == 47:
                            P.op("dve", lambda e, s=s: e.tensor_copy(out=hTh[:, :, :], in_=hT[s][:, :, 126:128]),
                                 reads=[("hT", s)], writes=["hTh"])
                        g2 = (t // 2) % 2
                        proj(t, wk, "wk", lambda ps, kps, cg, s=s: rope_evac(ps, kps, kb[s], ("kb", s), cg, s))
                        if own:
                            proj(t, wq, "wq", lambda ps, kps, cg, s=s: rope_evac(ps, kps, qb[s], ("qb", s), cg, s))
                        proj(t, wv, "wv", lambda ps, kps, cg, s=s: copy_op(
                            vb[s][:, cg * 512:(cg + 1) * 512], ps[:, :], [kps], [("vb", s)]))
                        P.dma("sp", lambda e, s=s, t=t: e.dma_start(out=v_scr[t * 128:(t + 1) * 128, :], in_=vb[s][:]),
                              reads=[("vb", s)])
                        transpose_bf(kb[s], ("kb", s), H, kTs[g2][:, :, (t % 2) * 128:(t % 2 + 1) * 128],
                                     ("kTs", g2), pbrot)
                        if t % 2 == 1:
                            grp = t // 2
                            P.dma("sp", lambda e, g2=g2, grp=grp: e.dma_start(
                                out=kT_scr[:, grp:grp + 1, :, :].rearrange("h g d t -> d (h g) t"), in_=kTs[g2][:]),
                                reads=[("kTs", g2)])
                        if own:
                            transpose_bf(qb[s], ("qb", s), H, qTs[g2][:, :, (t % 2) * 128:(t % 2 + 1) * 128],
                                         ("qTs", g2), pbrot)
                            if t % 2 == 1:
                                grp = (t - 48) // 2
                                P.dma("sp", lambda e, g2=g2, grp=grp: e.dma_start(
                                    out=qT_scr[:, grp:grp + 1, :, :].rearrange("h g d t -> d (h g) t"), in_=qTs[g2][:]),
                                    reads=[("qTs", g2)])

                    load_x(0)
                    load_rope(0)
                    load_x(1)
                    load_rope(1)
                    norm_tm(xs[0][:], ("xs", 0), gbc[:, 0, :], "gbc", hb[0][:], ("hb", 0))
                    for t in range(64):
                        do_tile(t)
                        if t + 2 < 64:
                            load_rope(t + 2)
                P.barrier()

        if "T" in phases:
            with contextlib.ExitStack() as sa:
                kT = [sb("kT%d" % i, [128, NBLK, 256], BF16, sa) for i in range(2)]
                Vt = [sb("Vt%d" % i, [128, 64, 130], BF16, sa) for i in range(2)]
                qT = [sb("qT%d" % i, [128, NOWN], BF16, sa) for i in range(2)]
                gbias = sb("gbias", [128, 16, NBLK], F32, sa)
                kms = sb("kms", [128, NBLK], F32, sa)
                kmb = sb("kmb", [128, NBLK], BF16, sa)
                gm = sb("gm", [128, 16, NBLK], F32, sa)
                m8 = sb("m8", [128, 16, 8], F32, sa)
                thr = sb("thr", [128, 16], F32, sa)
                sel = [sb("sel%d" % i, [128, 16, NBLK], F32, sa) for i in range(2)]
                pt = [sb("pt%d" % i, [128, 512], BF16, sa) for i in range(6)]
                pd = [sb("pd%d" % i, [128, 128], BF16, sa) for i in range(4)]
                oacc = [sb("oacc%d" % i, [128, 4, 129], F32, sa) for i in range(2)]
                rec = sb("rec", [128, 4], F32, sa)
                otmp = [sb("otmp%d" % i, [128, 132], F32, sa) for i in range(3)]
                otrot = Rot([(p, "otmp%d" % i) for i, p in enumerate(otmp)])
                ot = [sb("ot%d" % i, [128, 512], F32, sa) for i in range(2)]
                ats = [sb("ats%d" % i, [128, 512], BF16, sa) for i in range(2)]
                ps_s = [sa.enter_context(nc.psum_tensor("psS%d" % i, [128, 512], F32)) for i in range(2)]
                ps_o = [sa.enter_context(nc.psum_tensor("psO%d" % i, [128, 512], F32)) for i in range(6)]
                orot = Rot([(p, "psO%d" % i) for i, p in enumerate(ps_o)])
                srot = Rot([(p, "psS%d" % i) for i, p in enumerate(ps_s)])
                ptrot = Rot([(p, "pt%d" % i) for i, p in enumerate(pt)])
                pdrot = Rot([(p, "pd%d" % i) for i, p in enumerate(pd)])

                P.dma("sp", lambda e: e.dma_start(out=gbias[:], in_=gbias_d), writes=["gbias"])
                for i in range(2):
                    P.op("dve", lambda e, i=i: e.memset(Vt[i][:, :, 128:130], 1.0), writes=[("Vone", i)])

                def load_head(h):
                    b = h % 2
                    P.dma("sp", lambda e: e.dma_start(out=kT[b][:], in_=kT_scr[h:h + 1].rearrange("h g d t -> d (h g) t")),
                          writes=[("kT", b)])
                    P.dma("sp", lambda e: e.dma_start(
                        out=Vt[b][:, :, 0:128],
                        in_=v_scr[:, h * 128:(h + 1) * 128].rearrange("(t p) d -> p t d", p=128)),
                        writes=[("Vt", b)])
                    P.dma("sp", lambda e: e.dma_start(
                        out=qT[b][:].rearrange("p (g t) -> p g t", g=8),
                        in_=qT_scr[h:h + 1].rearrange("h g d t -> d (h g) t")), writes=[("qT", b)])

                obank_i = [0]

                def att_group(h, g, b, kTb, Vb, qTb, selb):
                    kkT, kV, kq, ksel = ("kT", b), ("Vt", b), ("qT", b), ("sel", b)
                    par = (h * 4 + g) % 2
                    oa = oacc[par]
                    koa4 = [("oacc", par, qi) for qi in range(4)]
                    emit_cast([P.op("dve", lambda e: e.memset(oa[:], 0.0), writes=koa4)])
                    qg = qTb[:, g * 512:(g + 1) * 512]
                    nA = NPAST + 2 * g
                    nB = nA + 1

                    def scores(n, ncols, q0):
                        res = []
                        for kt in range(2):
                            ps, kps = srot.next()
                            P.op("pe", lambda e, ps=ps, kt=kt: e.matmul(
                                ps[:, 0:ncols], kTb[:, n, kt * 128:(kt + 1) * 128], qg[:, q0:q0 + ncols],
                                start=True, stop=True), reads=[kkT, kq], writes=[kps])
                            p_, kp_ = ptrot.next()
                            P.op("act", lambda e, ps=ps, p_=p_: e.activation(
                                out=p_[:, 0:ncols], in_=ps[:, 0:ncols], func=AF.Exp, scale=SCALE),
                                reads=[kps], writes=[kp_])
                            res.append((p_, kp_))
                        return res

                    def pv_sel(n, pts, qis, col0):
                        for qi in qis:
                            bank, kbank = orot.next()
                            for kt in range(2):
                                p_, kp_ = pts[kt]
                                c = (qi - col0) * 128
                                P.op("pe", lambda e, bank=bank, p_=p_, kt=kt, c=c: e.matmul(
                                    bank[:, 0:129], p_[:, c:c + 128], Vb[:, 2 * n + kt, 0:129],
                                    start=(kt == 0), stop=(kt == 1)),
                                    reads=[kp_, kV, ("Vone", b)], writes=[kbank])
                            if qi == 3:
                                tm, ktm = otrot.next()
                                P.op("act", lambda e, bank=bank, qi=qi, tm=tm: e.activation(
                                    out=tm[:, 0:129], in_=bank[:, 0:129], func=AF.Identity,
                                    scale=selb[:, 4 * g + qi, n:n + 1]), reads=[kbank, ksel], writes=[ktm])
                                P.op("pool", lambda e, qi=qi, tm=tm: e.tensor_tensor(
                                    out=oa[:, qi, :], in0=tm[:, 0:129], in1=oa[:, qi, :], op=ALU.add),
                                    reads=[ktm, koa4[qi]], writes=[koa4[qi]])
                                continue
                            P.op("dve", lambda e, bank=bank, qi=qi: e.scalar_tensor_tensor(
                                out=oa[:, qi, :], in0=bank[:, 0:129], scalar=selb[:, 4 * g + qi, n:n + 1],
                                in1=oa[:, qi, :], op0=ALU.mult, op1=ALU.add),
                                reads=[kbank, ksel, koa4[qi]], writes=[koa4[qi]])

                    def pv_own(n, pts, qi, col0, two):
                        bank, kbank = orot.next()
                        c = (qi - col0) * 128
                        lst = []
                        if two:
                            lst.append((pts[0][0][:, c:c + 128], pts[0][1], 0))
                            dsrc, kd, dkt = pts[1][0], pts[1][1], 1
                        else:
                            dsrc, kd, dkt = pts[0][0], pts[0][1], 0
                        pdt, kpd = pdrot.next()
                        P.op("dve", lambda e: e.tensor_tensor(
                            out=pdt[:], in0=dsrc[:, c:c + 128], in1=trib[:], op=ALU.mult),
                            reads=[kd, "trib"], writes=[kpd])
                        lst.append((pdt[:], kpd, dkt))
                        nl = len(lst)
                        for i, (lh, klh, kt) in enumerate(lst):
                            P.op("pe", lambda e, lh=lh, kt=kt, i=i: e.matmul(
                                bank[:, 0:129], lh, Vb[:, 2 * n + kt, 0:129],
                                start=(i == 0), stop=(i == nl - 1)),
                                reads=[klh, kV, ("Vone", b)], writes=[kbank])
                        P.op("dve", lambda e: e.tensor_tensor(
                            out=oa[:, qi, :], in0=bank[:, 0:129], in1=oa[:, qi, :], op=ALU.add),
                            reads=[kbank, koa4[qi]], writes=[koa4[qi]])

                    prev = None
                    for n in range(nA):
                        cur = scores(n, 512, 0)
                        if prev is not None:
                            pv_sel(n - 1, prev, range(4), 0)
                        prev = cur
                    curA = scores(nA, 512, 0)
                    pv_sel(nA - 1, prev, range(4), 0)
                    curB = scores(nB, 256, 256)
                    pv_sel(nA, curA, (2, 3), 0)
                    pv_own(nA, curA, 0, 0, False)
                    pv_own(nA, curA, 1, 0, True)
                    pv_own(nB, curB, 2, 2, False)
                    pv_own(nB, curB, 3, 2, True)
                    P.op("dve", lambda e: e.reciprocal(out=rec[:, :], in_=oa[:, :, 128]),
                         reads=koa4, writes=["rec"])
                    o_ = ot[par]
                    ko = ("ot", par)
                    for qi in range(4):
                        P.op("dve", lambda e, qi=qi: e.tensor_scalar(
                            out=o_[:, qi * 128:(qi + 1) * 128], in0=oa[:, qi, 0:128], scalar1=rec[:, qi:qi + 1],
                            scalar2=None, op0=ALU.mult), reads=[koa4[qi], "rec"], writes=[ko])
                    tb, ktb = orot.next()
                    for qi in range(4):
                        P.op("pe", lambda e, qi=qi: e.transpose(
                            out=tb[:, qi * 128:(qi + 1) * 128], in_=o_[:, qi * 128:(qi + 1) * 128],
                            identity=idf[:]), reads=[ko, "idf"], writes=[ktb])
                    a_ = ats[par]
                    ka = ("ats", par)
                    copy_op(a_[:, :], tb[:, 0:512], [ktb], [ka])
                    P.dma("sp", lambda e: e.dma_start(
                        out=at_scr[h * 128:(h + 1) * 128, g * 512:(g + 1) * 512], in_=a_[:]), reads=[ka])

                def att_head(h):
                    b = h % 2
                    kTb, Vb, qTb, selb = kT[b], Vt[b], qT[b], sel[b]
                    kkT, kq, ksel = ("kT", b), ("qT", b), ("sel", b)
                    P.op("dve", lambda e: e.tensor_reduce(out=kms[:], in_=kTb[:], axis=AX.X, op=ALU.add),
                         reads=[kkT], writes=["kms"])
                    P.op("dve", lambda e: e.tensor_scalar(out=kmb[:], in0=kms[:], scalar1=1.0 / 256.0, scalar2=None,
                                                          op0=ALU.mult), reads=["kms"], writes=["kmb"])
                    ps_g, kpg = orot.next()
                    for qt in range(16):
                        P.op("pe", lambda e, qt=qt: e.matmul(
                            ps_g[:, qt * 32:(qt + 1) * 32], qTb[:, qt * 128:(qt + 1) * 128], kmb[:, :],
                            start=True, stop=True), reads=[kq, "kmb"], writes=[kpg])
                    P.op("dve", lambda e: e.tensor_tensor(out=gm[:].rearrange("p a b -> p (a b)"), in0=ps_g[:, :],
                                                          in1=gbias[:].rearrange("p a b -> p (a b)"), op=ALU.add),
                         reads=[kpg, "gbias"], writes=["gm"])
                    for qt in range(16):
                        P.op("dve", lambda e, qt=qt: e.max(out=m8[:, qt, :], in_=gm[:, qt, :]),
                             reads=["gm"], writes=["m8"])
                    P.op("dve", lambda e: e.tensor_scalar(out=thr[:], in0=m8[:, :, 2], scalar1=-1.0e29, scalar2=None,
                                                          op0=ALU.max), reads=["m8"], writes=["thr"])
                    for qt in range(16):
                        P.op("dve", lambda e, qt=qt: e.tensor_scalar(
                            out=selb[:, qt, :], in0=gm[:, qt, :], scalar1=thr[:, qt:qt + 1], scalar2=None,
                            op0=ALU.is_ge), reads=["gm", "thr"], writes=[ksel])
                    for g in range(4):
                        att_group(h, g, b, kTb, Vb, qTb, selb)

                load_head(0)
                for h in range(H):
                    if h + 1 < H:
                        load_head(h + 1)
                    att_head(h)
                while cast_i[0] < len(CASTS):
                    emit_cast([])
                P.barrier()

        if "B" in phases:
            with contextlib.ExitStack() as sa:
                xT = sb("xT", [128, KC, 512], F32, sa)
                hTb = sb("hTB", [128, KC, 512], BF16, sa)
                R = sb("R", [128, 24704], BF16, sa)
                pan = [sb("pan%d" % i, [128, KC * 512], BF16, sa) for i in range(4)]
                xsb2 = [sb("xsB%d" % i, [128, D], F32, sa) for i in range(2)]
                sqb = [sb("sqb%d" % i, [128, 2, 512], BF16, sa) for i in range(2)]
                rstd = sb("rstd", [128, 512], F32, sa)
                tmpf = [sb("tmpf%d" % i, [128, 512], F32, sa) for i in range(3)]
                px = [sb("px%d" % i, [128, 512], BF16, sa) for i in range(4)]
                hq = sb("hq", [128, XH, 512], BF16, sa)
                xo = sb("xo", [128, XH, 512], BF16, sa)
                cxh = sb("cxh", [128, 8, 2], F32, sa)
                uh = sb("uh", [128, 8, 2], BF16, sa)
                psf = [sa.enter_context(nc.psum_tensor("psfB%d" % i, [128, 512], F32)) for i in range(8)]
                pfrot = Rot([(p, "pfB%d" % i) for i, p in enumerate(psf)])
                panrot = Rot([(p, "pan%d" % i) for i, p in enumerate(pan)])
                tfrot = Rot([(p, "tmpf%d" % i) for i, p in enumerate(tmpf)])
                pxrot = Rot([(p, "px%d" % i) for i, p in enumerate(px)])

                cxs = R[:, 0:4096].rearrange("p (m t) -> p m t", m=8)
                ycT = R[:, 4096:8192].rearrange("p (m t) -> p m t", m=8)
                atT = R[:, 8192:12288].rearrange("p (m t) -> p m t", m=8)
                uc = R[:, 12288:16400].rearrange("p (m t) -> p m t", m=8)
                mg = R[:, 16512:24704].rearrange("p (m t) -> p m t", m=16)
                uT = R[:, 0:16384].rearrange("p (f t) -> p f t", f=32)
                RK = ["Rcxs", "Ryc", "Rat", "Ru"]

                def norm_fm(gi, out3, kout):
                    ps, kps = pfrot.next()
                    for q in range(8):
                        s = sqb[q % 2]
                        ks = ("sqb", q % 2)
                        P.op("act", lambda e, s=s, q=q: e.activation(out=s[:, :, :], in_=xT[:, 2 * q:2 * q + 2, :],
                                                                     func=AF.Square), reads=[("xT", 2 * q), ("xT", 2 * q + 1)], writes=[ks])
                        for j in range(2):
                            P.op("pe", lambda e, s=s, j=j, q=q, ps=ps: e.matmul(
                                ps[:, :], onesb[:, :], s[:, j, :], start=(q == 0 and j == 0),
                                stop=(q == 7 and j == 1)), reads=[ks, "onesb"], writes=[kps])
                    P.op("dve", lambda e, ps=ps: e.tensor_scalar(out=rstd[:], in0=ps[:, :], scalar1=1.0 / D, scalar2=EPS,
                                                                 op0=ALU.mult, op1=ALU.add), reads=[kps], writes=["rstd"])
                    P.op("act", lambda e: e.activation(out=rstd[:], in_=rstd[:], func=AF.Sqrt),
                         reads=["rstd"], writes=["rstd"])
                    P.op("dve", lambda e: e.reciprocal(out=rstd[:], in_=rstd[:]), reads=["rstd"], writes=["rstd"])
                    for kc in range(KC):
                        P.op("dve", lambda e, kc=kc: e.scalar_tensor_tensor(
                            out=out3[:, kc, :], in0=xT[:, kc, :], scalar=gT[:, gi, kc:kc + 1], in1=rstd[:],
                            op0=ALU.mult, op1=ALU.mult), reads=[("xT", kc), "gT", "rstd"], writes=[(kout, kc)])

                slot_i = [0]

                def get_panel(wap, r0, kcn, c0, ncol=512):
                    slot = slot_i[0] % len(PANELS)
                    slot_i[0] += 1
                    spec = PANELS[slot]
                    assert len(spec) == 1 and spec[0][1:5] == (r0, kcn, c0, ncol) and spec[0][0] is wap, (slot, spec)
                    p_, kp_ = panrot.next()
                    n = kcn * ncol
                    P.dma("pool", lambda e: e.dma_start(out=p_[:, 0:n], in_=wscr[slot, :, 0:n]), writes=[kp_])
                    return p_[:, 0:n].rearrange("p (k n) -> p k n", k=kcn), kp_

                def lin_fm(pv, kpv, kcn, m, rhs_fn, rkeys, ps, kps, ncols=512, first=True, last=True):
                    for kc in range(kcn):
                        P.op("pe", lambda e, kc=kc: e.matmul(
                            ps[:, 0:ncols], pv[:, kc, m * 128:(m + 1) * 128], rhs_fn(kc),
                            start=(first and kc == 0), stop=(last and kc == kcn - 1)),
                            reads=[kpv] + (rkeys(kc) if callable(rkeys) else rkeys), writes=[kps])

                for g in range(4):
                    for ti in range(4):
                        t = 48 + 4 * g + ti
                        xb_ = xsb2[ti % 2]
                        kxb = ("xsB", ti % 2)
                        P.dma("sp", lambda e, t=t, xb_=xb_: e.dma_start(out=xb_[:], in_=xc[t * 128:(t + 1) * 128, :]),
                              writes=[kxb])
                        for q in range(4):
                            ps, kps = pfrot.next()
                            for k in range(4):
                                P.op("pe", lambda e, ps=ps, k=k, q=q, xb_=xb_: e.transpose(
                                    out=ps[:, k * 128:(k + 1) * 128], in_=xb_[:, (4 * q + k) * 128:(4 * q + k + 1) * 128],
                                    identity=idf[:]), reads=[kxb, "idf"], writes=[kps])
                            copy_op(xT[:, 4 * q:4 * q + 4, ti * 128:(ti + 1) * 128],
                                    ps[:, :].rearrange("p (k t) -> p k t", k=4), [kps], [("xT", 4 * q + k) for k in range(4)])
                    norm_fm(0, hTb, "hT")
                    if g > 0:
                        P.op("dve", lambda e: e.tensor_copy(out=uc[:, :, 0:2], in_=uh[:, :, :]),
                             reads=["uh"], writes=["Ru"])
                    for pg in range(2):
                        pv, kpv = get_panel(w_in, 0, KC, 3072 + pg * 512)
                        for mm in range(4):
                            m = pg * 4 + mm
                            ps, kps = pfrot.next()
                            lin_fm(pv, kpv, KC, mm, lambda kc: hTb[:, kc, :], (lambda kc: [("hT", kc)]), ps, kps)
                            copy_op(cxs[:, m, :], ps[:, :], [kps], ["Rcxs"])
                            if g == 0:
                                ps2, kps2 = pfrot.next()
                                lin_fm(pv, kpv, KC, mm, lambda kc: hTh[:, kc, :], ["hTh"], ps2, kps2, ncols=2)
                                copy_op(cxh[:, m, :], ps2[:, 0:2], [kps2], ["cxh"])
                    for pg in range(2):
                        pv, kpv = get_panel(w_in, 0, KC, 5120 + pg * 512)
                        for mm in range(4):
                            m = pg * 4 + mm
                            ps, kps = pfrot.next()
                            lin_fm(pv, kpv, KC, mm, lambda kc: hTb[:, kc, :], (lambda kc: [("hT", kc)]), ps, kps)
                            P.op("dve", lambda e, ps=ps, m=m: e.tensor_tensor(
                                out=uc[:, m, 2:514], in0=ps[:, :], in1=cxs[:, m, :], op=ALU.mult),
                                reads=[kps, "Rcxs"], writes=["Ru"])
                            if g == 0:
                                ps2, kps2 = pfrot.next()
                                lin_fm(pv, kpv, KC, mm, lambda kc: hTh[:, kc, :], ["hTh"], ps2, kps2, ncols=2)
                                P.op("dve", lambda e, ps2=ps2, m=m: e.tensor_tensor(
                                    out=uc[:, m, 0:2], in0=ps2[:, 0:2], in1=cxh[:, m, :], op=ALU.mult),
                                    reads=[kps2, "cxh"], writes=["Ru"])
                    P.op("dve", lambda e: e.tensor_copy(out=uh[:, :, :], in_=uc[:, :, 512:514]),
                         reads=["Ru"], writes=["uh"])
                    for pg in range(2):
                        pv, kpv = get_panel(w_in, 0, KC, 4096 + pg * 512)
                        for mm in range(4):
                            m = pg * 4 + mm
                            ps, kps = pfrot.next()
                            lin_fm(pv, kpv, KC, mm, lambda kc: hTb[:, kc, :], (lambda kc: [("hT", kc)]), ps, kps)
                            y, ky = tfrot.next()
                            P.op("dve", lambda e, y=y, m=m: e.tensor_scalar(
                                out=y[:], in0=uc[:, m, 2:514], scalar1=cwT[:, m, 2:3], scalar2=None, op0=ALU.mult),
                                reads=["Ru", "cwT"], writes=[ky])
                            P.op("dve", lambda e, y=y, m=m: e.scalar_tensor_tensor(
                                out=y[:], in0=uc[:, m, 1:513], scalar=cwT[:, m, 1:2], in1=y[:],
                                op0=ALU.mult, op1=ALU.add), reads=["Ru", "cwT", ky], writes=[ky])
                            P.op("dve", lambda e, y=y, m=m: e.scalar_tensor_tensor(
                                out=y[:], in0=uc[:, m, 0:512], scalar=cwT[:, m, 0:1], in1=y[:],
                                op0=ALU.mult, op1=ALU.add), reads=["Ru", "cwT", ky], writes=[ky])
                            P.op("dve", lambda e, y=y, m=m, ps=ps: e.tensor_tensor(
                                out=ycT[:, m, :], in0=ps[:, :], in1=y[:], op=ALU.mult),
                                reads=[kps, ky], writes=["Ryc"])
                    P.dma("sp", lambda e, g=g: e.dma_start(
                        out=atT, in_=at_scr[:, g * 512:(g + 1) * 512].rearrange("(m p) t -> p m t", p=128)),
                        writes=["Rat"])
                    for cgp in range(4):
                        pga, kga = get_panel(w_in, 0, KC, 6144 + cgp * 512)
                        pgc, kgc = get_panel(w_in, 0, KC, 8192 + cgp * 512)
                        slot = slot_i[0] % len(PANELS)
                        slot_i[0] += 1
                        assert len(PANELS[slot]) == 2
                        p_, kp_ = panrot.next()
                        pao = p_[:, 0:4096].rearrange("p (k n) -> p k n", k=8)
                        pco = p_[:, 4096:8192].rearrange("p (k n) -> p k n", k=8)
                        P.dma("pool", lambda e, p_=p_, slot=slot: e.dma_start(out=p_[:, :], in_=wscr[slot, :, :]),
                              writes=[kp_])
                        for mm in range(4):
                            m = cgp * 4 + mm
                            psa, ka = pfrot.next()
                            lin_fm(pga, kga, KC, mm, lambda kc: hTb[:, kc, :], (lambda kc: [("hT", kc)]), psa, ka)
                            psc, kc_ = pfrot.next()
                            lin_fm(pgc, kgc, KC, mm, lambda kc: hTb[:, kc, :], (lambda kc: [("hT", kc)]), psc, kc_)
                            psy, ky_ = pfrot.next()
                            lin_fm(pao, kp_, 8, mm, lambda kc: atT[:, kc, :], ["Rat"], psy, ky_)
                            psz, kz_ = pfrot.next()
                            lin_fm(pco, kp_, 8, mm, lambda kc: ycT[:, kc, :], ["Ryc"], psz, kz_)
                            sa_, ksa = tfrot.next()
                            sc_, ksc = tfrot.next()
                            P.op("act", lambda e, sa_=sa_, psa=psa: e.activation(out=sa_[:], in_=psa[:, :], func=AF.Sigmoid),
                                 reads=[ka], writes=[ksa])
                            P.op("act", lambda e, sc_=sc_, psc=psc: e.activation(out=sc_[:], in_=psc[:, :], func=AF.Sigmoid),
                                 reads=[kc_], writes=[ksc])
                            P.op("dve", lambda e, sa_=sa_, psy=psy: e.tensor_tensor(out=sa_[:], in0=psy[:, :], in1=sa_[:], op=ALU.mult),
                                 reads=[ky_, ksa], writes=[ksa])
                            P.op("dve", lambda e, sc_=sc_, psz=psz: e.tensor_tensor(out=sc_[:], in0=psz[:, :], in1=sc_[:], op=ALU.mult),
                                 reads=[kz_, ksc], writes=[ksc])
                            P.op("dve", lambda e, sa_=sa_, sc_=sc_, m=m: e.tensor_tensor(out=mg[:, m, :], in0=sa_[:], in1=sc_[:], op=ALU.add),
                                 reads=[ksa, ksc], writes=["Rmg"])

                    def resid_layer(wap, kcn, rhs3, rkeys):
                        for cgp in range(4):
                            pv, kpv = get_panel(wap, 0, kcn, cgp * 512)
                            for mm in range(4):
                                m = cgp * 4 + mm
                                ps, kps = pfrot.next()
                                lin_fm(pv, kpv, kcn, mm, lambda kc: rhs3[:, kc, :], rkeys, ps, kps)
                                P.op("dve", lambda e, ps=ps, m=m: e.tensor_tensor(
                                    out=xT[:, m, :], in0=ps[:, :], in1=xT[:, m, :], op=ALU.add),
                                    reads=[kps, ("xT", m)], writes=[("xT", m)])

                    resid_layer(w_mix, KC, mg, ["Rmg"])
                    norm_fm(1, hTb, "hT")
                    pv, kpv = get_panel(wq_x, 0, KC, 0)
                    for hd in range(XH):
                        ps, kps = pfrot.next()
                        lin_fm(pv, kpv, KC, hd, lambda kc: hTb[:, kc, :], (lambda kc: [("hT", kc)]), ps, kps)
                        copy_op(hq[:, hd, :], ps[:, :], [kps], ["hq"])
                    for hd in range(XH):
                        pxs = []
                        for mt in range(2):
                            ps, kps = pfrot.next()
                            P.op("pe", lambda e, ps=ps, mt=mt, hd=hd: e.matmul(
                                ps[:, :], mkT[:, hd, mt * 128:(mt + 1) * 128], hq[:, hd, :], start=True, stop=True),
                                reads=["mkT", "hq"], writes=[kps])
                            p_, kp_ = pxrot.next()
                            P.op("act", lambda e, ps=ps, p_=p_: e.activation(out=p_[:], in_=ps[:, :], func=AF.Exp, scale=SCALE),
                                 reads=[kps], writes=[kp_])
                            pxs.append((p_, kp_))
                        pso, kpo = pfrot.next()
                        psz, kpz = pfrot.next()
                        for mt in range(2):
                            p_, kp_ = pxs[mt]
                            P.op("pe", lambda e, pso=pso, mt=mt, hd=hd, p_=p_: e.matmul(
                                pso[:, :], mv[:, mt, hd * 128:(hd + 1) * 128], p_[:], start=(mt == 0), stop=(mt == 1)),
                                reads=["mv", kp_], writes=[kpo])
                        for mt in range(2):
                            p_, kp_ = pxs[mt]
                            P.op("pe", lambda e, psz=psz, mt=mt, p_=p_: e.matmul(
                                psz[:, :], onesb[:, :], p_[:], start=(mt == 0), stop=(mt == 1)),
                                reads=["onesb", kp_], writes=[kpz])
                        rz, krz = tfrot.next()
                        P.op("dve", lambda e, rz=rz, psz=psz: e.reciprocal(out=rz[:], in_=psz[:, :]), reads=[kpz], writes=[krz])
                        P.op("dve", lambda e, rz=rz, pso=pso, hd=hd: e.tensor_tensor(
                            out=xo[:, hd, :], in0=pso[:, :], in1=rz[:], op=ALU.mult), reads=[kpo, krz], writes=["xo"])
                    resid_layer(wo_x, XH, xo, ["xo"])
                    norm_fm(2, hTb, "hT")
                    for half in range(2):
                        for pg in range(8):
                            pv, kpv = get_panel(w_up, 0, KC, half * 4096 + pg * 512)
                            for mm in range(4):
                                f = pg * 4 + mm
                                ps, kps = pfrot.next()
                                lin_fm(pv, kpv, KC, mm, lambda kc: hTb[:, kc, :], (lambda kc: [("hT", kc)]), ps, kps)
                                sq, ksq = tfrot.next()
                                P.op("act", lambda e, sq=sq, ps=ps: e.activation(out=sq[:], in_=ps[:, :], func=AF.Square),
                                     reads=[kps], writes=[ksq])
                                P.op("dve", lambda e, sq=sq, ps=ps, f=f: e.scalar_tensor_tensor(
                                    out=uT[:, f, :], in0=ps[:, :], scalar=0.0, in1=sq[:], op0=ALU.is_gt, op1=ALU.mult),
                                    reads=[kps, ksq], writes=[RK[f // 8]])
                        for cgp in range(4):
                            banks = [pfrot.next() for _ in range(4)]
                            for qd in range(2):
                                pv, kpv = get_panel(w_down, half * 4096 + qd * 2048, KC, cgp * 512)
                                for mm in range(4):
                                    ps, kps = banks[mm]
                                    lin_fm(pv, kpv, KC, mm, lambda kc, qd=qd: uT[:, qd * 16 + kc, :],
                                           [RK[2 * qd], RK[2 * qd + 1]], ps, kps, first=(qd == 0), last=(qd == 1))
                            for mm in range(4):
                                m = cgp * 4 + mm
                                ps, kps = banks[mm]
                                P.op("dve", lambda e, ps=ps, m=m: e.tensor_tensor(
                                    out=xT[:, m, :], in0=ps[:, :], in1=xT[:, m, :], op=ALU.add),
                                    reads=[kps, ("xT", m)], writes=[("xT", m)])
                    norm_fm(3, xT, "xT")
                    for ti in range(4):
                        xb_ = xsb2[ti % 2]
                        kxb = ("xsB", ti % 2)
                        for q in range(4):
                            ps, kps = pfrot.next()
                            for k in range(4):
                                P.op("pe", lambda e, ps=ps, k=k, q=q, ti=ti: e.transpose(
                                    out=ps[:, k * 128:(k + 1) * 128], in_=xT[:, 4 * q + k, ti * 128:(ti + 1) * 128],
                                    identity=idf[:]), reads=[("xT", 4 * q + k), "idf"], writes=[kps])
                            copy_op(xb_[:, q * 512:(q + 1) * 512], ps[:, :], [kps], [kxb])
                        r0 = g * 512 + ti * 128
                        P.dma("sp", lambda e, r0=r0, xb_=xb_: e.dma_start(out=out_d[r0:r0 + 128, :], in_=xb_[:]), reads=[kxb])

        P.finalize()
        esems = {en: es.enter_context(nc.semaphore("es_" + en)) for en in P.ENGS}
        dsems = {k: es.enter_context(nc.semaphore("ds_%s_%d" % k)) for k in P.dma_uses}
        block = es.enter_context(nc.Block())

        @block.sync
        def _(e):
            P.emit_engine("sp", e, esems, dsems)

        @block.scalar
        def _(e):
            P.emit_engine("act", e, esems, dsems)

        @block.vector
        def _(e):
            P.emit_engine("dve", e, esems, dsems)

        @block.gpsimd
        def _(e):
            P.emit_engine("pool", e, esems, dsems)

        @block.tensor
        def _(e):
            P.emit_engine("pe", e, esems, dsems)
    return nc


def _consts(j):
    half = 16
    inv_freq = (500000.0 ** (-np.arange(half, dtype=np.float32) / half)).astype(np.float32)
    pos = (np.arange(NCTX) + (j - 3) * NOWN).astype(np.float32)
    ang = pos[:, None] * inv_freq[None, :]
    cos, sin = np.cos(ang).astype(np.float32), np.sin(ang).astype(np.float32)
    rope = np.concatenate([np.tile(cos, (1, 4)), np.tile(sin, (1, 4))], axis=1).astype(np.float32)
    gb = np.full((16, NBLK), NEG, np.float32)
    first_valid = NPAST - 8 * j
    for qt in range(16):
        ob = NPAST + qt // 2
        gb[qt, first_valid:ob] = 0.0
    gbias = np.ascontiguousarray(np.broadcast_to(gb[None], (128, 16, NBLK))).astype(np.float32)
    return rope, gbias


_NC_CACHE = {}


def make_in_maps(inputs):
    f = lambda k: np.ascontiguousarray(np.asarray(inputs[k], dtype=np.float32))
    x, mem = f("x"), f("mem")
    shared = {
        "w_in": f("w_in")[0], "w_attn_out": f("w_attn_out")[0], "w_conv_out": f("w_conv_out")[0],
        "w_mix_out": f("w_mix_out")[0], "wq_x": f("wq_x")[0], "wkv_x": f("wkv_x")[0], "wo_x": f("wo_x")[0],
        "w_up": f("w_up")[0], "w_down": f("w_down")[0],
    }
    gbc = np.stack([np.broadcast_to(f("norm_mix")[0][None], (128, D)),
                    np.broadcast_to(f("norm_mem")[0][None], (128, D))]).astype(np.float32)
    gT = np.stack([f("norm_mix")[0], f("norm_xattn")[0], f("norm_mlp")[0], f("norm_final")], 0)
    gT = np.ascontiguousarray(gT.reshape(4, KC, 128).transpose(2, 0, 1))
    cwT = np.ascontiguousarray(f("conv_w")[0].reshape(3, 8, 128).transpose(2, 1, 0))
    shared.update({"gbc": np.ascontiguousarray(gbc), "gT": gT, "cwT": cwT,
                   "ident": np.eye(128, dtype=np.float32),
                   "tri": np.triu(np.ones((128, 128), np.float32))})
    in_maps = []
    for c in range(8):
        b, j = c // 4, c % 4
        xcw = np.zeros((NCTX, D), np.float32)
        n = (j + 1) * NOWN
        xcw[NCTX - n:] = x[b, :n]
        rope, gbias = _consts(j)
        m = dict(shared)
        m.update({"xc": xcw, "mem": mem[b], "rope": rope, "gbias": gbias})
        in_maps.append(m)
    return in_maps


def kernel(**inputs):
    if "nc" not in _NC_CACHE:
        _NC_CACHE["nc"] = build()
    nc = _NC_CACHE["nc"]
    in_maps = make_in_maps(inputs)
    res = run_bass_kernel_spmd(nc, in_maps, core_ids=list(range(8)))
    out = np.zeros((2, 8192, D), np.float32)
    for c in range(8):
        b, j = c // 4, c % 4
        out[b, j * NOWN:(j + 1) * NOWN] = np.asarray(res.results[c]["out"], dtype=np.float32)
    return out
```
